# Optimizing a Trainium2 kernel written in Bass

```python
import math
import jax, jax.numpy as jnp
from jax import lax
import numpy as np

D_MODEL = 2048
BATCH = 8
SEQ = 2048
DEPTH = 1

SSM_WIDTH = D_MODEL // 2
SSM_GROUP = 16
SSM_GROUPS = SSM_WIDTH // SSM_GROUP
SSM_STATE = 64
DT_MIN = 1e-3
DT_MAX = 1e-1
CONV_WIDTH = D_MODEL // 2
CONV_K = 3
D_FF = 5504
EPS = 1e-6
IN_COLS = SSM_WIDTH + 3 * CONV_WIDTH + 2 * D_MODEL

kernel_name = "hybrid_s5_shortconv_macaron_block"


def _rmsnorm(x, g):
    xf = x.astype(jnp.float32)
    y = xf * lax.rsqrt(jnp.mean(xf * xf, axis=-1, keepdims=True) + EPS)
    return (y * g.astype(jnp.float32)).astype(x.dtype)


def _swiglu(x, w_gate, w_up, w_down):
    return (jax.nn.silu(x @ w_gate) * (x @ w_up)) @ w_down


def _s5_discretize(lam_re, lam_im, log_dt, b_re, b_im):
    lam_re = jnp.minimum(lam_re, -1e-4)
    dt = jnp.exp(log_dt)[:, None]
    mag = jnp.exp(lam_re * dt)
    a_re = mag * jnp.cos(lam_im * dt)
    a_im = mag * jnp.sin(lam_im * dt)
    den = lam_re * lam_re + lam_im * lam_im
    p = a_re - 1.0
    f_re = (p * lam_re + a_im * lam_im) / den
    f_im = (a_im * lam_re - p * lam_im) / den
    f_re = f_re[:, :, None]
    f_im = f_im[:, :, None]
    bb_re = f_re * b_re - f_im * b_im
    bb_im = f_re * b_im + f_im * b_re
    return a_re, a_im, bb_re, bb_im


def _ssm_combine(left, right):
    a1r, a1i, b1r, b1i = left
    a2r, a2i, b2r, b2i = right
    ar = a2r * a1r - a2i * a1i
    ai = a2r * a1i + a2i * a1r
    br = a2r * b1r - a2i * b1i + b2r
    bi = a2r * b1i + a2i * b1r + b2i
    return ar, ai, br, bi


def _s5_branch(v, lam_re, lam_im, log_dt, b_re, b_im, c_re, c_im, d_skip, w_glu, b_glu):
    bsz, seq, _ = v.shape
    vf = v.astype(jnp.float32).reshape(bsz, seq, SSM_GROUPS, SSM_GROUP)
    a_re, a_im, bb_re, bb_im = _s5_discretize(lam_re, lam_im, log_dt, b_re, b_im)
    bu_re = jnp.einsum('bsgc,gnc->bsgn', vf, bb_re)
    bu_im = jnp.einsum('bsgc,gnc->bsgn', vf, bb_im)
    shp = (1, seq, SSM_GROUPS, SSM_STATE)
    ar = jnp.broadcast_to(a_re[None, None], shp)
    ai = jnp.broadcast_to(a_im[None, None], shp)
    _, _, s_re, s_im = lax.associative_scan(_ssm_combine, (ar, ai, bu_re, bu_im), axis=1)
    y = (jnp.einsum('bsgn,gcn->bsgc', s_re, c_re)
         - jnp.einsum('bsgn,gcn->bsgc', s_im, c_im))
    y = y.reshape(bsz, seq, SSM_WIDTH) + d_skip * vf.reshape(bsz, seq, SSM_WIDTH)
    y = jax.nn.gelu(y)
    y = y * jax.nn.sigmoid(y @ w_glu + b_glu)
    return y.astype(v.dtype)


def _short_conv_branch(b_gate, c_gate, val, conv_w, conv_b):
    seq = val.shape[1]
    z = c_gate * val
    zp = jnp.pad(z, ((0, 0), (CONV_K - 1, 0), (0, 0)))
    conv = conv_b + sum(conv_w[k] * zp[:, k:k + seq] for k in range(CONV_K))
    return b_gate * conv


def setup_inputs(seed: int = 0) -> dict:
    key = jax.random.key(seed)
    ks = jax.random.split(key, 32)
    f32 = jnp.float32
    D, F, W, CW, G, N, C = D_MODEL, D_FF, SSM_WIDTH, CONV_WIDTH, SSM_GROUPS, SSM_STATE, SSM_GROUP

    def nrm(k, shape, scale):
        return jax.random.normal(k, shape, f32) * scale

    def gain(k, n):
        return 1.0 + 0.02 * jax.random.normal(k, (n,), f32)

    return {
        "x": jax.random.normal(ks[0], (BATCH, SEQ, D), f32),
        "ffn1_norm": gain(ks[1], D),
        "ffn1_w_gate": nrm(ks[2], (D, F), D ** -0.5),
        "ffn1_w_up": nrm(ks[3], (D, F), D ** -0.5),
        "ffn1_w_down": nrm(ks[4], (F, D), F ** -0.5),
        "mix_norm": gain(ks[5], D),
        "w_in": nrm(ks[6], (D, IN_COLS), D ** -0.5),
        "ssm_lambda_re": -0.5 + 0.01 * jax.random.normal(ks[7], (G, N), f32),
        "ssm_lambda_im": math.pi * jnp.broadcast_to(jnp.arange(N, dtype=f32), (G, N))
                         + 0.01 * jax.random.normal(ks[8], (G, N), f32),
        "ssm_log_dt": jax.random.uniform(ks[9], (G,), f32, math.log(DT_MIN), math.log(DT_MAX)),
        "ssm_b_re": nrm(ks[10], (G, N, C), (2 * C) ** -0.5),
        "ssm_b_im": nrm(ks[11], (G, N, C), (2 * C) ** -0.5),
        "ssm_c_re": nrm(ks[12], (G, C, N), N ** -0.5),
        "ssm_c_im": nrm(ks[13], (G, C, N), N ** -0.5),
        "ssm_d": nrm(ks[14], (W,), 1.0),
        "ssm_w_glu": nrm(ks[15], (W, W), W ** -0.5),
        "ssm_b_glu": nrm(ks[16], (W,), 0.01),
        "ssm_w_out": nrm(ks[17], (W, D), W ** -0.5),
        "conv_w": nrm(ks[18], (CONV_K, CW), CONV_K ** -0.5),
        "conv_b": nrm(ks[19], (CW,), 0.01),
        "conv_w_out": nrm(ks[20], (CW, D), CW ** -0.5),
        "w_o": nrm(ks[21], (D, D), D ** -0.5),
        "ffn2_norm": gain(ks[22], D),
        "ffn2_w_gate": nrm(ks[23], (D, F), D ** -0.5),
        "ffn2_w_up": nrm(ks[24], (D, F), D ** -0.5),
        "ffn2_w_down": nrm(ks[25], (F, D), F ** -0.5),
        "final_norm": gain(ks[26], D),
    }


def reference(x, ffn1_norm, ffn1_w_gate, ffn1_w_up, ffn1_w_down, mix_norm, w_in,
              ssm_lambda_re, ssm_lambda_im, ssm_log_dt, ssm_b_re, ssm_b_im, ssm_c_re, ssm_c_im,
              ssm_d, ssm_w_glu, ssm_b_glu, ssm_w_out, conv_w, conv_b, conv_w_out, w_o,
              ffn2_norm, ffn2_w_gate, ffn2_w_up, ffn2_w_down, final_norm):
    h = x
    for _ in range(DEPTH):
        h = h + 0.5 * _swiglu(_rmsnorm(h, ffn1_norm), ffn1_w_gate, ffn1_w_up, ffn1_w_down)
        u = _rmsnorm(h, mix_norm)
        proj = u @ w_in
        splits = [SSM_WIDTH, SSM_WIDTH + CONV_WIDTH, SSM_WIDTH + 2 * CONV_WIDTH,
                  SSM_WIDTH + 3 * CONV_WIDTH, SSM_WIDTH + 3 * CONV_WIDTH + D_MODEL]
        v_ssm, b_gate, c_gate, val, ga_pre, gb_pre = jnp.split(proj, splits, axis=-1)
        y_a = _s5_branch(v_ssm, ssm_lambda_re, ssm_lambda_im, ssm_log_dt, ssm_b_re, ssm_b_im,
                         ssm_c_re, ssm_c_im, ssm_d, ssm_w_glu, ssm_b_glu)
        y_b = _short_conv_branch(b_gate, c_gate, val, conv_w, conv_b)
        z_a = y_a @ ssm_w_out
        z_b = y_b @ conv_w_out
        merged = jax.nn.sigmoid(ga_pre) * z_a + jax.nn.sigmoid(gb_pre) * z_b
        h = h + merged @ w_o
        h = h + 0.5 * _swiglu(_rmsnorm(h, ffn2_norm), ffn2_w_gate, ffn2_w_up, ffn2_w_down)
    return _rmsnorm(h, final_norm)
```

```python
import numpy as np
import concourse.bass as bass
import concourse.mybir as mybir
from concourse.bass_utils import run_bass_kernel_spmd

F32 = mybir.dt.float32
BF16 = mybir.dt.bfloat16
I32 = mybir.dt.int32
AF = mybir.ActivationFunctionType
ALU = mybir.AluOpType

D = 2048
S = 2048
FF = 5504
NF = FF // 128
TC = 512
NTC = S // TC
W = 1024
G = 64
NS = 64
TWO_PI = float(2 * np.pi)


class T:
    __slots__ = ("ap", "w", "r", "name", "dsem", "dcnt", "off", "size")

    def __init__(self, ap, name=""):
        self.ap = ap
        self.w = None
        self.r = {}
        self.name = name
        self.dsem = None
        self.dcnt = 0
        self.off = None
        self.size = None


class Prog:
    ENG = ("pe", "act", "dve", "pool", "sp")
    EPOCH = 30000

    def __init__(self, nc):
        self.nc = nc
        self.q = {e: [] for e in self.ENG}
        self.seen = {e: {} for e in self.ENG}
        self.sems = []
        self.esem = {e: None for e in self.ENG}
        self.ecnt = {e: 0 for e in self.ENG}
        self.dsems = {}

    def newsem(self, name):
        h = self.nc.alloc_semaphore(name=name)
        self.sems.append(h)
        return len(self.sems) - 1

    def _engev(self, e):
        if self.esem[e] is None or self.ecnt[e] >= self.EPOCH:
            self.esem[e] = self.newsem(f"s_{e}_{len(self.sems)}")
            self.ecnt[e] = 0
        self.ecnt[e] += 1
        return (self.esem[e], self.ecnt[e])

    def _waits(self, eng, reads, writes):
        evs = {}
        for t in reads:
            if t.w is not None and evs.get(t.w[0], 0) < t.w[1]:
                evs[t.w[0]] = t.w[1]
        for t in writes:
            if t.w is not None and evs.get(t.w[0], 0) < t.w[1]:
                evs[t.w[0]] = t.w[1]
            for s, v in t.r.items():
                if evs.get(s, 0) < v:
                    evs[s] = v
        waits = []
        seen = self.seen[eng]
        for s, v in evs.items():
            if eng == "pe" and s == self.esem["pe"]:
                continue
            if seen.get(s, 0) >= v:
                continue
            seen[s] = v
            waits.append((s, v))
        return waits

    def _commit(self, ev, reads, writes):
        s, v = ev
        for t in writes:
            t.w = ev
            t.r = {}
        for t in reads:
            if t.r.get(s, 0) < v:
                t.r[s] = v

    def op(self, eng, fn, reads=(), writes=()):
        waits = self._waits(eng, reads, writes)
        ev = self._engev(eng)
        self.q[eng].append((waits, fn, (ev[0], 1)))
        self._commit(ev, reads, writes)
        return ev

    def dma(self, eng, fn, reads=(), writes=(), sem_tile=None):
        waits = self._waits(eng, reads, writes)
        st = sem_tile if sem_tile is not None else (writes[0] if writes else reads[0])
        if st.name not in self.dsems:
            self.dsems[st.name] = [self.newsem(f"d_{st.name}"), 0]
        rec = self.dsems[st.name]
        rec[1] += 16
        ev = (rec[0], rec[1])
        self.q[eng].append((waits, fn, (rec[0], 16)))
        self._commit(ev, reads, writes)
        return ev

    def wait_all(self, eng, evs):
        waits = []
        seen = self.seen[eng]
        for s, v in evs:
            if seen.get(s, 0) >= v:
                continue
            seen[s] = v
            waits.append((s, v))
        self.q[eng].append((waits, None, None))

    def emit(self):
        nc = self.nc
        sems = self.sems
        with nc.Block() as block:
            def mk(e):
                def body(eng):
                    for waits, fn, inc in self.q[e]:
                        for s, v in waits:
                            eng.wait_ge(sems[s], v)
                        if fn is not None:
                            ins = fn(eng)
                            if inc is not None:
                                ins.then_inc(sems[inc[0]], inc[1])
                return body
            block.tensor(mk("pe"))
            block.scalar(mk("act"))
            block.vector(mk("dve"))
            block.gpsimd(mk("pool"))
            block.sync(mk("sp"))


def build(stage="full"):
    nc = bass.Bass("TRN2", target_bir_lowering=False)
    P = Prog(nc)

    def din(name, shape, dt=F32):
        return nc.dram_tensor(name, list(shape), dt, kind="ExternalInput").ap()

    x = din("x", [S, D])
    w1g = din("w1g", [D, FF]); w1u = din("w1u", [D, FF]); w1d = din("w1d", [FF, D])
    w2g = din("w2g", [D, FF]); w2u = din("w2u", [D, FF]); w2d = din("w2d", [FF, D])
    w_in = din("w_in", [D, 8192])
    w_glu = din("w_glu", [W, W])
    w_o = din("w_o", [D, D])
    w_cv = din("w_cv", [D, 3072])
    w_gate = din("w_gate", [D, 4096])
    w_oc = din("w_oc", [W, 4096])
    vec_d = din("vec", [128, 128])
    gfin_d = din("gfin", [128, D])
    lam_d = din("lam", [128, 3, 64])
    BT_d = din("BT", [128, 64, 2, 128])
    CD_d = din("CD", [128, 64, 2, 128])
    ident_d = din("ident", [128, 128])
    perm_d = din("perm", [128, 128])
    iota_d = din("iota", [128, TC])
    out = nc.dram_tensor("out", [S, D], F32, kind="ExternalOutput").ap()
    Cs_d = nc.dram_tensor("Cs_scr", [128, 64, 2, 128], BF16, kind="Internal").ap()
    TAB_d = nc.dram_tensor("TAB_scr", [128, 64, 2, TC], F32, kind="Internal").ap()
    _tcs = [T(None, f"Cs_d{b}") for b in range(2)]
    T_Cs = [_tcs[b // 4] for b in range(8)]
    _ttab = [T(None, f"TAB_d{g}") for g in range(4)]
    T_TAB = [_ttab[gp // 8] for gp in range(32)]

    def sb(name, shape, dt):
        return nc.sbuf_tensor(name, list(shape), dt).__enter__()

    pb = []
    for i in range(8):
        t = nc.psum_tensor(f"pb{i}", [128, 512], F32).__enter__()
        pb.append(T(t, f"pb{i}"))

    def pbf(i):
        return pb[i].ap[:].bitcast(BF16)

    h_t = [sb(f"h{i}", [128, D], F32) for i in range(4)]
    Th = [T(h_t[i], f"h{i}") for i in range(4)]
    uT_t = sb("uT", [128, 16, TC], BF16)
    TuT = [T(uT_t, f"uT{i}") for i in range(4)]
    NG = 7
    gsto = sb("gsto", [128, NG * 2, TC], BF16)
    Tgsto = [T(gsto, f"gsto{i}") for i in range(NG)]
    NSLOT = 5
    SLOT = 6144
    ring_t = [sb(f"ring{i}", [128, SLOT], BF16) for i in range(NSLOT)]
    Tring = [[T(ring_t[i], f"ring{i}_{r}") for r in range(4)] for i in range(NSLOT)]
    for tl_ in Tring:
        for t_ in tl_:
            P.dsems[t_.name] = [P.newsem(f"d_{t_.name}"), 0]
    vec = sb("vecs", [128, 128], F32); Tvec = T(vec, "vec")
    ident_f = sb("ident_f", [128, 128], F32); Tidf = T(ident_f, "idf")
    ident_b = sb("ident_b", [128, 128], BF16); Tidb = T(ident_b, "idb")
    perm = sb("perm_s", [128, 128], F32); Tperm = T(perm, "perm")
    iota = sb("iota_s", [128, TC], F32); Tiota = T(iota, "iota")
    sm = sb("small", [128, 24, 64], F32)
    Tsm = [T(sm, f"sm{i}") for i in range(24)]
    stat = sb("stat", [128, 16], F32); Tstat = [T(stat, f"stat{i}") for i in range(4)]
    cst = sb("cst", [128, 4], F32); Tcst = T(cst, "cst")
    zcar = sb("zcar", [128, 8, 2], F32); Tzcar = [T(zcar, f"zcar{i}") for i in range(8)]
    ARENA = 74 * 1024 // 2
    arena = sb("arena", [128, ARENA], BF16)

    C_G1, C_GM, C_G2, C_CW, C_CB, C_SD, C_BG, C_SGN, C_NSGN, C_SC2PI, C_NSC2PI = 0, 16, 32, 48, 72, 80, 88, 96, 97, 98, 99
    (I_LRE, I_LIM, I_LDT, I_DT, I_TH, I_R, I_T0, I_T1, I_T2, I_T3, I_FRE, I_FIM, I_FA, I_FB, I_FA2, I_FB2,
     I_COSL, I_SINL, I_LAST, I_INIT, I_T4, I_T5, I_T6, I_T7) = range(24)

    arena_tiles = []

    class Phase:
        def __init__(self, base=0):
            self.ptr = base

        def alloc(self, name, shape, dt):
            n = 1
            for s_ in shape[1:]:
                n *= s_
            nbytes = n * (2 if dt == BF16 else 4)
            nb16 = (nbytes + 63) // 64 * 32
            off = self.ptr
            self.ptr += nb16
            assert self.ptr <= ARENA, (name, self.ptr, ARENA)
            ap = arena[:, off:off + nbytes // 2]
            if dt != BF16:
                ap = ap.bitcast(dt)
            if len(shape) == 3:
                ap = ap.rearrange("p (a b) -> p a b", a=shape[1])
            elif len(shape) == 4:
                ap = ap.rearrange("p (a b c) -> p a b c", a=shape[1], b=shape[2])
            t = T(ap, name)
            t.off, t.size = off, nb16
            for o in arena_tiles:
                if o.off < off + nb16 and off < o.off + o.size:
                    if o.w is not None and t.r.get(o.w[0], 0) < o.w[1]:
                        t.r[o.w[0]] = o.w[1]
                    for s_, v in o.r.items():
                        if t.r.get(s_, 0) < v:
                            t.r[s_] = v
            arena_tiles.append(t)
            return t

    class WS:
        def __init__(self):
            self.loads = []
            self.wts = []
            self.issued = 0
            self.cur = -1

        def add(self, fn, wt=128):
            self.loads.append(fn)
            self.wts.append(wt)

        def get(self, inuse=2):
            self.cur += 1
            k = self.cur
            while self.issued < len(self.loads) and self.issued <= k + NSLOT - inuse:
                if self.issued > k and sum(self.wts[k:self.issued + 1]) > 1200:
                    break
                self.loads[self.issued](self.issued % NSLOT)
                self.issued += 1
            return k % NSLOT

    ws = WS()

    def wload(slot, dst_ap, src_ap, sub=None):
        if sub is None:
            P.dma("pool", lambda e: e.dma_start(out=dst_ap, in_=src_ap, max_dma_last_dim=2048), writes=Tring[slot], sem_tile=Tring[slot][0])
        else:
            P.dma("pool", lambda e: e.dma_start(out=dst_ap, in_=src_ap, max_dma_last_dim=2048), writes=[Tring[slot][sub]])

    def ld_A(wd, m0, mw, KC=16):
        def f(slot):
            dst = ring_t[slot][:, 0:KC * mw].rearrange("p (kc m) -> p kc m", kc=KC)
            src = wd.rearrange("(kc p) m -> p kc m", p=128)[:, :, m0:m0 + mw]
            wload(slot, dst, src)
        return f

    def ld_B(wd, f0, nfc, c0, cw):
        def f(slot):
            dst = ring_t[slot][:, 0:nfc * cw].rearrange("p (fc c) -> p fc c", fc=nfc)
            src = wd[f0 * 128:(f0 + nfc) * 128, :].rearrange("(fc p) c -> p fc c", p=128)[:, :, c0:c0 + cw]
            wload(slot, dst, src)
        return f

    FBLK = [(0, 12), (12, 12), (24, 12), (36, 7)]

    def ffn_loads(wg, wu, wd):
        for (f0, nf) in FBLK:
            sub = []
            k = 0
            while k < nf:
                n = min(3, nf - k)
                sub.append((f0 + k, n))
                k += n
            for (fs, n) in sub:
                ws.add(ld_A(wg, fs * 128, n * 128))
                ws.add(ld_A(wu, fs * 128, n * 128))
            for j in range(4):
                ws.add(ld_B(wd, f0, nf, j * 512, 512))

    VSUB = [(0, 3), (3, 3), (6, 2)]

    def ld_bt(cb):
        def f(slot):
            dst = ring_t[slot][:, 0:8 * 2 * 128].rearrange("p (g v n) -> p g v n", g=8, v=2)
            wload(slot, dst, BT_d[:, cb * 8:(cb + 1) * 8, :, :])
        return f

    def ld_conv(cb):
        def f(slot):
            dst = ring_t[slot][:, 0:16 * 384].rearrange("p (kc m) -> p kc m", kc=16)
            wload(slot, dst, w_cv.rearrange("(kc p) m -> p kc m", p=128)[:, :, cb * 384:(cb + 1) * 384])
        return f

    def ld_m5(i):
        def f(slot):
            r = ring_t[slot]
            d0 = r[:, 0:8 * 256].rearrange("p (kc m) -> p kc m", kc=8)
            wload(slot, d0, w_oc.rearrange("(kc p) m -> p kc m", p=128)[:, :, i * 256:(i + 1) * 256], sub=0)
            d1 = r[:, 2048:2048 + 16 * 256].rearrange("p (kc m) -> p kc m", kc=16)
            wload(slot, d1, w_gate.rearrange("(kc p) m -> p kc m", p=128)[:, :, i * 256:(i + 1) * 256], sub=1)
        return f

    def ld_oc(i):
        def f(slot):
            d0 = ring_t[slot][:, 0:8 * 256].rearrange("p (kc m) -> p kc m", kc=8)
            wload(slot, d0, w_oc.rearrange("(kc p) m -> p kc m", p=128)[:, :, i * 256:(i + 1) * 256])
        return f

    def mixer_loads():
        for cb in range(8):
            ws.add(ld_A(w_in, cb * 128, 128))
            ws.add(ld_bt(cb), 32)
            ws.add(ld_conv(cb), 128)
            if cb < NG:
                ws.add(ld_A(w_gate, cb * 256, 256), 128)
        for (m0, n) in VSUB:
            ws.add(ld_A(w_glu, m0 * 128, n * 128, KC=8), 64)
        for i in range(16):
            if i < NG:
                ws.add(ld_oc(i), 64)
            else:
                ws.add(ld_m5(i), 192)
        for j in range(4):
            ws.add(ld_B(w_o, 0, 8, j * 512, 512), 64)
            ws.add(ld_B(w_o, 8, 8, j * 512, 512), 64)

    do_ffn1 = stage in ("full", "ffn1", "ffn1mix")
    do_mix = stage in ("full", "mix", "ffn1mix")
    do_setup_only = stage == "setup"
    do_ffn2 = stage in ("full",)
    for tc in range(NTC):
        if do_ffn1:
            ffn_loads(w1g, w1u, w1d)
        if do_mix:
            mixer_loads()
        if do_ffn2:
            ffn_loads(w2g, w2u, w2d)

    P.dma("sp", lambda e: e.dma_start(out=vec[:], in_=vec_d), writes=[Tvec])
    P.dma("sp", lambda e: e.dma_start(out=ident_f[:], in_=ident_d), writes=[Tidf])
    P.op("act", lambda e: e.activation(out=ident_b[:], in_=ident_f[:], func=AF.Copy), reads=[Tidf], writes=[Tidb])
    P.op("dve", lambda e: e.memset(cst[:, 0:1], 1e-6), writes=[Tcst])
    P.op("dve", lambda e: e.memset(cst[:, 1:2], 0.0), writes=[Tcst])

    def col(c):
        return vec[:, c:c + 1]

    def prefetch_x(tc):
        ph = Phase(base=8192)
        xs = [ph.alloc(f"xs{i}", [128, D], F32) for i in range(4)]
        for i in range(4):
            r0 = tc * TC + i * 128
            P.dma("sp", lambda e, i=i, r0=r0: e.dma_start(out=xs[i].ap, in_=x[r0:r0 + 128, :]), writes=[xs[i]])
        return xs

    def rstd_for(i, junk):
        P.op("act", lambda e: e.activation(out=junk.ap, in_=h_t[i][:], func=AF.Square, accum_out=stat[:, 4 * i:4 * i + 1]),
             reads=[Th[i]], writes=[junk, Tstat[i]])
        P.op("act", lambda e: e.activation(out=stat[:, 4 * i + 1:4 * i + 2], in_=stat[:, 4 * i:4 * i + 1], func=AF.Sqrt,
                                           scale=1.0 / D, bias=cst[:, 0:1]),
             reads=[Tstat[i], Tcst], writes=[Tstat[i]])
        P.op("dve", lambda e: e.reciprocal(out=stat[:, 4 * i + 2:4 * i + 3], in_=stat[:, 4 * i + 1:4 * i + 2]),
             reads=[Tstat[i]], writes=[Tstat[i]])

    def norm_T(gcol0, src=None):
        ph = Phase(base=0)
        xq = [ph.alloc(f"xq{i}", [128, D], BF16) for i in range(4)]
        if src is None:
            sap = [h_t[i][:] for i in range(4)]
            sT = Th
        else:
            sap = [t.ap for t in src]
            sT = src
        for i in range(4):
            if i < 2:
                P.op("dve", lambda e, i=i: e.scalar_tensor_tensor(out=xq[i].ap, in0=sap[i], scalar=1.0, in1=sap[i], op0=ALU.mult, op1=ALU.mult,
                                                                 accum_out=stat[:, 4 * i:4 * i + 1]),
                     reads=[sT[i]], writes=[xq[i], Tstat[i]])
            else:
                P.op("act", lambda e, i=i: e.activation(out=xq[i].ap, in_=sap[i], func=AF.Square, accum_out=stat[:, 4 * i:4 * i + 1]),
                     reads=[sT[i]], writes=[xq[i], Tstat[i]])
        for i in range(4):
            P.op("act", lambda e, i=i: e.activation(out=stat[:, 4 * i + 1:4 * i + 2], in_=stat[:, 4 * i:4 * i + 1], func=AF.Sqrt,
                                                    scale=1.0 / D, bias=cst[:, 0:1]),
                 reads=[Tstat[i], Tcst], writes=[Tstat[i]])
            P.op("dve", lambda e, i=i: e.reciprocal(out=stat[:, 4 * i + 2:4 * i + 3], in_=stat[:, 4 * i + 1:4 * i + 2]),
                 reads=[Tstat[i]], writes=[Tstat[i]])
        for i in range(4):
            P.op("act", lambda e, i=i: e.activation(out=xq[i].ap, in_=sap[i], func=AF.Copy, scale=stat[:, 4 * i + 2:4 * i + 3]),
                 reads=[sT[i], Tstat[i]], writes=[xq[i]])
            for half in range(2):
                bank = 6 + half
                pv = pbf(bank)
                for k in range(8):
                    kc = half * 8 + k
                    P.op("pe", lambda e, pv=pv, k=k, kc=kc, i=i: e.transpose(pv[:, k * 128:(k + 1) * 128], xq[i].ap[:, kc * 128:(kc + 1) * 128], ident_b[:]),
                         reads=[xq[i], Tidb], writes=[pb[bank]])
                g0 = gcol0 + half * 8
                P.op("dve", lambda e, pv=pv, half=half, i=i, g0=g0: e.tensor_tensor(
                        out=uT_t[:, half * 8:(half + 1) * 8, i * 128:(i + 1) * 128],
                        in0=pv.rearrange("p (k m) -> p k m", k=8),
                        in1=vec[:, g0:g0 + 8].unsqueeze(2).broadcast_to([128, 8, 128]), op=ALU.mult),
                     reads=[pb[bank], Tvec], writes=[TuT[i]])

    def ffn(side_work=None, src=None, drain_first=False):
        ph = Phase()
        first_blk = True
        act = ph.alloc("act", [128, 12, TC], BF16)
        sg = [ph.alloc(f"sg{i}", [128, TC], F32) for i in range(2)]
        Tact = [T(act.ap, f"act{i}") for i in range(12)]
        for t in Tact:
            t.r = dict(act.r)
        cnt = 0
        dn = 0
        for (f0, nf) in FBLK:
            k = 0
            while k < nf:
                n = min(3, nf - k)
                sg_slot = ws.get()
                su_slot = ws.get()
                wgv = ring_t[sg_slot][:, 0:16 * n * 128].rearrange("p (kc m) -> p kc m", kc=16)
                wuv = ring_t[su_slot][:, 0:16 * n * 128].rearrange("p (kc m) -> p kc m", kc=16)
                for q in range(n):
                    fl = k + q
                    gb_, ub_ = cnt % 2, 2 + cnt % 2
                    for kc in range(16):
                        P.op("pe", lambda e, gb_=gb_, wgv=wgv, q=q, kc=kc: e.matmul(pb[gb_].ap[:], wgv[:, kc, q * 128:(q + 1) * 128], uT_t[:, kc, :], start=(kc == 0), stop=(kc == 15)),
                             reads=Tring[sg_slot] + TuT, writes=[pb[gb_]])
                    for kc in range(16):
                        P.op("pe", lambda e, ub_=ub_, wuv=wuv, q=q, kc=kc: e.matmul(pb[ub_].ap[:], wuv[:, kc, q * 128:(q + 1) * 128], uT_t[:, kc, :], start=(kc == 0), stop=(kc == 15)),
                             reads=Tring[su_slot] + TuT, writes=[pb[ub_]])
                    s_ = sg[cnt % 2]
                    P.op("act", lambda e, s_=s_, gb_=gb_: e.activation(out=s_.ap, in_=pb[gb_].ap[:], func=AF.Silu), reads=[pb[gb_]], writes=[s_])
                    P.op("dve", lambda e, s_=s_, ub_=ub_, fl=fl: e.tensor_tensor(out=act.ap[:, fl, :], in0=pb[ub_].ap[:], in1=s_.ap, op=ALU.mult),
                         reads=[pb[ub_], s_], writes=[Tact[fl]])
                    cnt += 1
                    for _ in range(3):
                        if side_work is not None:
                            try:
                                next(side_work)
                            except StopIteration:
                                side_work = None
                k += n
            if first_blk and side_work is not None and drain_first:
                for _ in side_work:
                    pass
                side_work = None
            for j in range(4):
                sd_slot = ws.get()
                wdv = ring_t[sd_slot][:, 0:nf * 512].rearrange("p (fc c) -> p fc c", fc=nf)
                for i in range(4):
                    db = 4 + dn % 2
                    dn += 1
                    for fl in range(nf):
                        P.op("pe", lambda e, db=db, fl=fl, i=i, wdv=wdv, nf=nf: e.matmul(pb[db].ap[:], act.ap[:, fl, i * 128:(i + 1) * 128], wdv[:, fl, :], start=(fl == 0), stop=(fl == nf - 1)),
                             reads=[Tact[fl]] + Tring[sd_slot], writes=[pb[db]])
                    if first_blk and src is not None:
                        rs_ap, rs_T = src[i].ap[:, j * 512:(j + 1) * 512], src[i]
                    else:
                        rs_ap, rs_T = h_t[i][:, j * 512:(j + 1) * 512], Th[i]
                    P.op("dve", lambda e, db=db, i=i, j=j, rs_ap=rs_ap: e.scalar_tensor_tensor(out=h_t[i][:, j * 512:(j + 1) * 512], in0=pb[db].ap[:], scalar=0.5,
                                                                             in1=rs_ap, op0=ALU.mult, op1=ALU.add),
                         reads=[pb[db], rs_T], writes=[Th[i]])
            first_blk = False
        for t in Tact:
            if t.w is not None and act.r.get(t.w[0], 0) < t.w[1]:
                act.r[t.w[0]] = t.w[1]
            for s_, v in t.r.items():
                if act.r.get(s_, 0) < v:
                    act.r[s_] = v
        if side_work is not None:
            for _ in side_work:
                pass

    out_evs = []

    def final_gen(tc):
        ph = Phase(base=24576)
        gf = ph.alloc("gfin", [128, D], F32)
        ob = [ph.alloc(f"ob{i}", [128, D], F32) for i in range(2)]
        P.dma("sp", lambda e: e.dma_start(out=gf.ap, in_=gfin_d), writes=[gf])
        for i in range(4):
            o = ob[i % 2]
            rstd_for(i, o)
            yield
            P.op("dve", lambda e, i=i, o=o: e.scalar_tensor_tensor(out=o.ap, in0=h_t[i][:], scalar=stat[:, 4 * i + 2:4 * i + 3], in1=gf.ap,
                                                                  op0=ALU.mult, op1=ALU.mult),
                 reads=[Th[i], Tstat[i], gf], writes=[o])
            r0 = tc * TC + i * 128
            ev = P.dma("sp", lambda e, o=o, r0=r0: e.dma_start(out=out[r0:r0 + 128, :], in_=o.ap), reads=[o], sem_tile=o)
            out_evs.append(ev)
            yield

    def smv(i):
        return sm[:, i, :]

    def ssm_setup():
        S_ = Tsm
        P.dma("sp", lambda e: e.dma_start(out=sm[:, 0:3, :], in_=lam_d), writes=[S_[I_LRE], S_[I_LIM], S_[I_LDT]])
        P.dma("sp", lambda e: e.dma_start(out=perm[:], in_=perm_d), writes=[Tperm])
        P.dma("sp", lambda e: e.dma_start(out=iota[:], in_=iota_d), writes=[Tiota])
        P.op("dve", lambda e: e.memset(smv(I_INIT), 0.0), writes=[S_[I_INIT]])
        P.op("dve", lambda e: e.memset(zcar[:].rearrange("p a b -> p (a b)"), 0.0), writes=Tzcar)
        P.op("act", lambda e: e.activation(out=smv(I_DT), in_=smv(I_LDT), func=AF.Exp), reads=[S_[I_LDT]], writes=[S_[I_DT]])
        P.op("dve", lambda e: e.tensor_scalar(out=smv(I_LRE), in0=smv(I_LRE), scalar1=-1e-4, scalar2=None, op0=ALU.min), reads=[S_[I_LRE]], writes=[S_[I_LRE]])
        P.op("dve", lambda e: e.tensor_tensor(out=smv(I_T0), in0=smv(I_LRE), in1=smv(I_DT), op=ALU.mult), reads=[S_[I_LRE], S_[I_DT]], writes=[S_[I_T0]])
        P.op("act", lambda e: e.activation(out=smv(I_R), in_=smv(I_T0), func=AF.Exp), reads=[S_[I_T0]], writes=[S_[I_R]])
        P.op("dve", lambda e: e.scalar_tensor_tensor(out=smv(I_TH), in0=smv(I_LIM), scalar=1.0 / TWO_PI, in1=smv(I_DT), op0=ALU.mult, op1=ALU.mult),
             reads=[S_[I_LIM], S_[I_DT]], writes=[S_[I_TH]])
        yield

        def trig(dst_i, src_i, mul, add, scale_ap=None):
            P.op("dve", lambda e: e.tensor_scalar(out=smv(I_T4), in0=smv(src_i), scalar1=float(mul), scalar2=float(add), op0=ALU.mult, op1=ALU.add),
                 reads=[S_[src_i]], writes=[S_[I_T4]])
            reduce_turns(smv(I_T4), smv(I_T5).bitcast(I32), smv(I_T6), [S_[I_T4]], [S_[I_T5]], [S_[I_T6]])
            P.op("act", lambda e: e.activation(out=smv(dst_i), in_=smv(I_T4), func=AF.Sin, scale=TWO_PI), reads=[S_[I_T4]], writes=[S_[dst_i]])

        trig(I_T1, I_TH, 1.0, 8.25)
        trig(I_T2, I_TH, 1.0, 8.0)
        trig(I_COSL, I_TH, float(TC), 8.25)
        trig(I_SINL, I_TH, float(TC), 8.0)
        yield
        P.op("dve", lambda e: e.tensor_tensor(out=smv(I_T1), in0=smv(I_T1), in1=smv(I_R), op=ALU.mult), reads=[S_[I_T1], S_[I_R]], writes=[S_[I_T1]])
        P.op("dve", lambda e: e.tensor_tensor(out=smv(I_T2), in0=smv(I_T2), in1=smv(I_R), op=ALU.mult), reads=[S_[I_T2], S_[I_R]], writes=[S_[I_T2]])
        P.op("dve", lambda e: e.tensor_scalar(out=smv(I_T1), in0=smv(I_T1), scalar1=-1.0, scalar2=None, op0=ALU.add), reads=[S_[I_T1]], writes=[S_[I_T1]])
        P.op("dve", lambda e: e.tensor_tensor(out=smv(I_T3), in0=smv(I_LRE), in1=smv(I_LRE), op=ALU.mult), reads=[S_[I_LRE]], writes=[S_[I_T3]])
        P.op("dve", lambda e: e.tensor_tensor(out=smv(I_T4), in0=smv(I_LIM), in1=smv(I_LIM), op=ALU.mult), reads=[S_[I_LIM]], writes=[S_[I_T4]])
        P.op("dve", lambda e: e.tensor_tensor(out=smv(I_T3), in0=smv(I_T3), in1=smv(I_T4), op=ALU.add), reads=[S_[I_T3], S_[I_T4]], writes=[S_[I_T3]])
        P.op("dve", lambda e: e.reciprocal(out=smv(I_T3), in_=smv(I_T3)), reads=[S_[I_T3]], writes=[S_[I_T3]])
        P.op("dve", lambda e: e.tensor_tensor(out=smv(I_T4), in0=smv(I_T1), in1=smv(I_LRE), op=ALU.mult), reads=[S_[I_T1], S_[I_LRE]], writes=[S_[I_T4]])
        P.op("dve", lambda e: e.tensor_tensor(out=smv(I_T5), in0=smv(I_T2), in1=smv(I_LIM), op=ALU.mult), reads=[S_[I_T2], S_[I_LIM]], writes=[S_[I_T5]])
        P.op("dve", lambda e: e.tensor_tensor(out=smv(I_T4), in0=smv(I_T4), in1=smv(I_T5), op=ALU.add), reads=[S_[I_T4], S_[I_T5]], writes=[S_[I_T4]])
        P.op("dve", lambda e: e.tensor_tensor(out=smv(I_FRE), in0=smv(I_T4), in1=smv(I_T3), op=ALU.mult), reads=[S_[I_T4], S_[I_T3]], writes=[S_[I_FRE]])
        P.op("dve", lambda e: e.tensor_tensor(out=smv(I_T4), in0=smv(I_T2), in1=smv(I_LRE), op=ALU.mult), reads=[S_[I_T2], S_[I_LRE]], writes=[S_[I_T4]])
        P.op("dve", lambda e: e.tensor_tensor(out=smv(I_T5), in0=smv(I_T1), in1=smv(I_LIM), op=ALU.mult), reads=[S_[I_T1], S_[I_LIM]], writes=[S_[I_T5]])
        P.op("dve", lambda e: e.tensor_tensor(out=smv(I_T4), in0=smv(I_T4), in1=smv(I_T5), op=ALU.subtract), reads=[S_[I_T4], S_[I_T5]], writes=[S_[I_T4]])
        P.op("dve", lambda e: e.tensor_tensor(out=smv(I_FIM), in0=smv(I_T4), in1=smv(I_T3), op=ALU.mult), reads=[S_[I_T4], S_[I_T3]], writes=[S_[I_FIM]])
        lo, hi = slice(0, 64), slice(64, 128)
        rd = [S_[I_FRE], S_[I_FIM]]
        P.op("dve", lambda e: e.tensor_scalar(out=smv(I_T6), in0=smv(I_FIM), scalar1=-1.0, scalar2=None, op0=ALU.mult), reads=rd, writes=[S_[I_T6]])
        P.op("dve", lambda e: e.tensor_scalar(out=smv(I_T7), in0=smv(I_FRE), scalar1=-1.0, scalar2=None, op0=ALU.mult), reads=rd, writes=[S_[I_T7]])
        rd2 = rd + [S_[I_T6], S_[I_T7]]
        P.op("dve", lambda e: e.tensor_copy(out=sm[lo, I_FA, :], in_=sm[lo, I_FRE, :]), reads=rd2, writes=[S_[I_FA]])
        P.op("dve", lambda e: e.tensor_copy(out=sm[hi, I_FA, :], in_=sm[hi, I_T6, :]), reads=rd2, writes=[S_[I_FA]])
        P.op("dve", lambda e: e.tensor_copy(out=sm[lo, I_FB, :], in_=sm[lo, I_T6, :]), reads=rd2, writes=[S_[I_FB]])
        P.op("dve", lambda e: e.tensor_copy(out=sm[hi, I_FB, :], in_=sm[hi, I_T7, :]), reads=rd2, writes=[S_[I_FB]])
        P.op("dve", lambda e: e.tensor_copy(out=sm[lo, I_FA2, :], in_=sm[lo, I_T6, :]), reads=rd2, writes=[S_[I_FA2]])
        P.op("dve", lambda e: e.tensor_copy(out=sm[hi, I_FA2, :], in_=sm[hi, I_FRE, :]), reads=rd2, writes=[S_[I_FA2]])
        P.op("dve", lambda e: e.tensor_copy(out=sm[lo, I_FB2, :], in_=sm[lo, I_T7, :]), reads=rd2, writes=[S_[I_FB2]])
        P.op("dve", lambda e: e.tensor_copy(out=sm[hi, I_FB2, :], in_=sm[hi, I_T6, :]), reads=rd2, writes=[S_[I_FB2]])
        yield
        ph = Phase(base=ARENA - 20 * 1024 // 2)
        cin = ph.alloc("cin", [128, 8, 2, 128], F32)
        cout = ph.alloc("cout", [128, 8, 2, 128], BF16)
        ctmp = ph.alloc("ctmp", [128, 128], F32)
        for b in range(8):
            P.dma("sp", lambda e, b=b: e.dma_start(out=cin.ap, in_=CD_d[:, b * 8:(b + 1) * 8, :, :]), writes=[cin])
            for gl in range(8):
                g = b * 8 + gl
                for v, (ia, ib) in enumerate(((I_FA, I_FB), (I_FA2, I_FB2))):
                    P.op("dve", lambda e, gl=gl, g=g, ib=ib: e.tensor_scalar(out=ctmp.ap, in0=cin.ap[:, gl, 1, :], scalar1=sm[:, ib, g:g + 1], scalar2=None, op0=ALU.mult),
                         reads=[cin, S_[ib]], writes=[ctmp])
                    P.op("dve", lambda e, gl=gl, g=g, ia=ia, v=v: e.scalar_tensor_tensor(out=cout.ap[:, gl, v, :], in0=cin.ap[:, gl, 0, :], scalar=sm[:, ia, g:g + 1],
                                                                                   in1=ctmp.ap, op0=ALU.mult, op1=ALU.add),
                         reads=[cin, S_[ia], ctmp], writes=[cout])
                if gl % 2 == 1:
                    yield
            P.dma("sp", lambda e, b=b: e.dma_start(out=Cs_d[:, b * 8:(b + 1) * 8, :, :], in_=cout.ap), reads=[cout], writes=[T_Cs[b]], sem_tile=T_Cs[b])
        ph = Phase(base=ARENA - 40 * 1024 // 2)
        q = ph.alloc("tq", [128, 2, 2, TC], F32)
        ki = ph.alloc("tki", [128, 2, 2, TC], I32)
        g1 = ph.alloc("tg1", [128, 2, 2, TC], F32)
        tb = [ph.alloc(f"ttb{i}", [128, 2, 2, TC], F32) for i in range(2)]
        fl = lambda t: t.ap.rearrange("p a b c -> p (a b c)")
        for gp in range(32):
            g0 = gp * 2
            for gl in range(2):
                g = g0 + gl
                P.op("pool", lambda e, gl=gl, g=g: e.tensor_scalar(out=q.ap[:, gl, 0, :], in0=iota[:], scalar1=sm[:, I_TH, g:g + 1], scalar2=8.25, op0=ALU.mult, op1=ALU.add),
                     reads=[Tiota, S_[I_TH]], writes=[q])
                P.op("pool", lambda e, gl=gl, g=g: e.tensor_scalar(out=q.ap[:, gl, 1, :], in0=iota[:], scalar1=sm[:, I_TH, g:g + 1], scalar2=8.0, op0=ALU.mult, op1=ALU.add),
                     reads=[Tiota, S_[I_TH]], writes=[q])
            P.op("dve", lambda e: e.tensor_scalar(out=fl(g1), in0=fl(q), scalar1=12582912.0, scalar2=12582912.0, op0=ALU.add, op1=ALU.subtract), reads=[q], writes=[g1])
            yield
            P.op("dve", lambda e: e.tensor_tensor(out=fl(q), in0=fl(q), in1=fl(g1), op=ALU.subtract), reads=[q, g1], writes=[q])
            yield
            t_ = tb[gp % 2]
            P.op("act", lambda e, t_=t_: e.activation(out=t_.ap[:, :, 0, :], in_=q.ap[:, :, 0, :], func=AF.Sin, scale=TWO_PI), reads=[q], writes=[t_])
            P.op("act", lambda e, t_=t_: e.activation(out=t_.ap[:, :, 1, :], in_=q.ap[:, :, 1, :], func=AF.Sin, scale=col(C_SC2PI)), reads=[q, Tvec], writes=[t_])
            P.dma("sp", lambda e, t_=t_, g0=g0: e.dma_start(out=TAB_d[:, g0:g0 + 2, :, :], in_=t_.ap), reads=[t_], writes=[T_TAB[gp]], sem_tile=T_TAB[gp])
            yield

    def reduce_turns(qa, kia, g1a, Tq, Tki, Tg1):
        P.op("dve", lambda e: e.tensor_copy(out=kia, in_=qa), reads=Tq, writes=Tki)
        P.op("dve", lambda e: e.tensor_tensor(out=qa, in0=qa, in1=kia, op=ALU.subtract), reads=Tq + Tki, writes=Tq)
        P.op("dve", lambda e: e.scalar_tensor_tensor(out=g1a, in0=qa, scalar=0.5, in1=qa, op0=ALU.is_gt, op1=ALU.subtract), reads=Tq, writes=Tg1)
        P.op("dve", lambda e: e.scalar_tensor_tensor(out=qa, in0=g1a, scalar=0.5, in1=g1a, op0=ALU.is_gt, op1=ALU.subtract), reads=Tg1, writes=Tq)

    def mixer(tc):
        S_ = Tsm
        ph = Phase()
        yabf = ph.alloc("yabf", [128, 8, TC], BF16)
        yb = ph.alloc("yb", [128, 8, TC], BF16)
        Tya = [T(yabf.ap, f"ya{i}") for i in range(8)]
        Tyb = [T(yb.ap, f"yb{i}") for i in range(8)]
        for t in Tya:
            t.r = dict(yabf.r)
        for t in Tyb:
            t.r = dict(yb.r)
        base2 = ph.ptr
        v32 = [ph.alloc(f"v32_{i}", [128, TC], F32) for i in range(2)]
        vbf = [ph.alloc(f"vbf_{i}", [128, TC], BF16) for i in range(2)]
        tabs = [ph.alloc(f"tab{i}", [128, 2, TC], F32) for i in range(3)]
        t1 = [ph.alloc(f"t1_{i}", [128, TC], F32) for i in range(2)]
        t2 = [ph.alloc(f"t2_{i}", [128, TC], F32) for i in range(2)]
        Z = [ph.alloc(f"Z_{i}", [128, TC], F32) for i in range(2)]
        W1 = [ph.alloc(f"W1_{i}", [128, TC], BF16) for i in range(2)]
        W2 = [ph.alloc(f"W2_{i}", [128, TC], BF16) for i in range(2)]
        Cs = [ph.alloc(f"Cs_{i}", [128, 8, 2, 128], BF16) for i in range(2)]
        ysum = [ph.alloc(f"ysum{i}", [128, TC], F32) for i in range(2)]
        gtmp = [ph.alloc(f"gtmp{i}", [128, TC], F32) for i in range(2)]
        csb = ph.alloc("csb", [128, TC], F32)
        zt = ph.alloc("zt", [128, TC + 2], F32)
        cacc = ph.alloc("cacc", [128, TC], F32)
        CB = (6, 0, 7)
        CORD = (1, 2, 0)
        st = {}

        def setup_batch(cb):
            vslot = ws.get()
            vview = ring_t[vslot][:, 0:16 * 128].rearrange("p (kc m) -> p kc m", kc=16)
            for kc in range(16):
                P.op("pe", lambda e, kc=kc, vview=vview: e.matmul(pb[0].ap[:], vview[:, kc, 0:128], uT_t[:, kc, :], start=(kc == 0), stop=(kc == 15)),
                     reads=Tring[vslot] + TuT, writes=[pb[0]])
            vv, vb_ = v32[cb % 2], vbf[cb % 2]
            P.op("act", lambda e, vv=vv: e.activation(out=vv.ap, in_=pb[0].ap[:], func=AF.Copy), reads=[pb[0]], writes=[vv])
            P.op("act", lambda e, vb_=vb_, vv=vv: e.activation(out=vb_.ap, in_=vv.ap, func=AF.Copy), reads=[vv], writes=[vb_])
            bslot = ws.get()
            btv = ring_t[bslot][:, 0:8 * 2 * 128].rearrange("p (g v n) -> p g v n", g=8, v=2)
            cslot = ws.get()
            cv = ring_t[cslot][:, 0:16 * 384].rearrange("p (kc s m) -> p kc s m", kc=16, s=3)
            gslot, gtv = None, None
            if cb < NG:
                gslot = ws.get(inuse=3)
                gtv = ring_t[gslot][:, 0:16 * 256].rearrange("p (kc m) -> p kc m", kc=16)
            cs_ = Cs[cb % 2]
            P.dma("sp", lambda e, cs_=cs_, cb=cb: e.dma_start(out=cs_.ap, in_=Cs_d[:, cb * 8:(cb + 1) * 8, :, :]), reads=[T_Cs[cb]], writes=[cs_])
            st[cb] = dict(vv=vv, vb=vb_, bslot=bslot, btv=btv, cslot=cslot, cv=cv, cs=cs_, gslot=gslot, gtv=gtv)

        def stageA_pe(G, with_conv=True):
            cb, gl = G // 8, G % 8
            d = st[cb]
            vb_, btv, bslot = d["vb"], d["btv"], d["bslot"]
            tab = tabs[G % 3]
            P.dma("sp", lambda e, tab=tab, G=G: e.dma_start(out=tab.ap, in_=TAB_d[:, G, :, :]), reads=[T_TAB[G // 2]], writes=[tab])
            x1b, x2b = 1, 2
            P.op("pe", lambda e: e.matmul(pb[x1b].ap[:], btv[:, gl, 0, :], vb_.ap, start=True, stop=True), reads=Tring[bslot] + [vb_], writes=[pb[x1b]])
            P.op("pe", lambda e: e.matmul(pb[x2b].ap[:], btv[:, gl, 1, :], vb_.ap, start=True, stop=True), reads=Tring[bslot] + [vb_], writes=[pb[x2b]])

        def stageA_conv(G):
            cb, gl = G // 8, G % 8
            d = st[cb]
            cv, cslot = d["cv"], d["cslot"]
            for ci in range(gl * 6, gl * 6 + 6):
                s_i, kc = CORD[ci // 16], ci % 16
                P.op("pe", lambda e, s_i=s_i, kc=kc: e.matmul(pb[CB[s_i]].ap[:], cv[:, kc, s_i, :], uT_t[:, kc, :], start=(kc == 0), stop=(kc == 15)),
                     reads=Tring[cslot] + TuT, writes=[pb[CB[s_i]]])
                if ci == 15:
                    P.op("act", lambda e: e.activation(out=csb.ap, in_=pb[0].ap[:], func=AF.Copy), reads=[pb[0]], writes=[csb])
            if cb < NG:
                gtv, gslot = d["gtv"], d["gslot"]
                ab, bank = (0, 3) if gl < 4 else (1, 4)
                for kc in range((gl % 4) * 4, (gl % 4) * 4 + 4):
                    P.op("pe", lambda e, kc=kc: e.matmul(pb[bank].ap[:], gtv[:, kc, ab * 128:(ab + 1) * 128], uT_t[:, kc, :], start=(kc == 0), stop=(kc == 15)),
                         reads=Tring[gslot] + TuT, writes=[pb[bank]])
                if gl % 4 == 3:
                    P.op("act", lambda e: e.activation(out=gsto[:, 2 * cb + ab, :], in_=pb[bank].ap[:], func=AF.Sigmoid), reads=[pb[bank]], writes=[Tgsto[cb]])

        def stageA_dve(G):
            tab = tabs[G % 3]
            x1b, x2b = 1, 2
            a1, a2 = t1[G % 2], t2[G % 2]
            P.op("dve", lambda e: e.tensor_tensor(out=a1.ap, in0=pb[x1b].ap[:], in1=tab.ap[:, 0, :], op=ALU.mult), reads=[pb[x1b], tab], writes=[a1])
            P.op("dve", lambda e: e.tensor_tensor(out=a2.ap, in0=pb[x2b].ap[:], in1=tab.ap[:, 1, :], op=ALU.mult), reads=[pb[x2b], tab], writes=[a2])

        def stageA_add(G):
            a1, a2 = t1[G % 2], t2[G % 2]
            P.op("dve", lambda e: e.tensor_tensor(out=a1.ap, in0=a1.ap, in1=a2.ap, op=ALU.add), reads=[a1, a2], writes=[a1])

        def stageB_ew(G):
            tab = tabs[G % 3]
            a1, z_, w1_, w2_ = t1[G % 2], Z[G % 2], W1[G % 2], W2[G % 2]
            P.op("dve", lambda e: e.tensor_tensor_scan(out=z_.ap, data0=sm[:, I_R, G:G + 1].broadcast_to([128, TC]), data1=a1.ap,
                                                       initial=sm[:, I_INIT, G:G + 1], op0=ALU.mult, op1=ALU.add),
                 reads=[a1, S_[I_R], S_[I_INIT]], writes=[z_])
            P.op("act", lambda e: e.activation(out=sm[:, I_LAST, G:G + 1], in_=z_.ap[:, TC - 1:TC], func=AF.Copy), reads=[z_], writes=[S_[I_LAST]])
            P.op("pool", lambda e: e.tensor_tensor(out=w2_.ap, in0=z_.ap, in1=tab.ap[:, 1, :], op=ALU.mult), reads=[z_, tab], writes=[w2_])
            P.op("pool", lambda e: e.tensor_tensor(out=w1_.ap, in0=z_.ap, in1=tab.ap[:, 0, :], op=ALU.mult), reads=[z_, tab], writes=[w1_])

        def stageB_pe(G):
            cb, gl = G // 8, G % 8
            cs_ = st[cb]["cs"]
            w1_, w2_ = W1[G % 2], W2[G % 2]
            P.op("pe", lambda e: e.matmul(pb[5].ap[:], cs_.ap[:, gl, 0, :], w1_.ap, start=(gl == 0), stop=False), reads=[cs_, w1_], writes=[pb[5]])
            P.op("pe", lambda e: e.matmul(pb[5].ap[:], cs_.ap[:, gl, 1, :], w2_.ap, start=False, stop=(gl == 7)), reads=[cs_, w2_], writes=[pb[5]])

        def epilogue(cb):
            vv = st[cb]["vv"]
            ys, gt = ysum[cb % 2], gtmp[cb % 2]
            P.op("dve", lambda e: e.scalar_tensor_tensor(out=ys.ap, in0=vv.ap, scalar=col(C_SD + cb), in1=pb[5].ap[:], op0=ALU.mult, op1=ALU.add),
                 reads=[vv, Tvec, pb[5]], writes=[ys])
            P.op("pool", lambda e: e.tensor_tensor(out=gt.ap, in0=ys.ap, in1=ys.ap, op=ALU.mult), reads=[ys], writes=[gt])
            P.op("pool", lambda e: e.tensor_scalar(out=gt.ap, in0=gt.ap, scalar1=0.044715, scalar2=1.0, op0=ALU.mult, op1=ALU.add), reads=[gt], writes=[gt])
            P.op("pool", lambda e: e.tensor_tensor(out=gt.ap, in0=gt.ap, in1=ys.ap, op=ALU.mult), reads=[ys, gt], writes=[gt])
            P.op("act", lambda e: e.activation(out=gt.ap, in_=gt.ap, func=AF.Sigmoid, scale=1.5957691216057308), reads=[gt], writes=[gt])
            P.op("pool", lambda e: e.tensor_tensor(out=yabf.ap[:, cb, :], in0=ys.ap, in1=gt.ap, op=ALU.mult), reads=[ys, gt], writes=[Tya[cb]])
            P.op("act", lambda e: e.activation(out=zt.ap[:, 0:2], in_=zcar[:, cb, :], func=AF.Copy), reads=[Tzcar[cb]], writes=[zt])
            P.op("dve", lambda e: e.tensor_tensor(out=zt.ap[:, 2:TC + 2], in0=pb[7].ap[:], in1=csb.ap, op=ALU.mult), reads=[pb[7], csb, zt], writes=[zt])
            P.op("act", lambda e: e.activation(out=zcar[:, cb, :], in_=zt.ap[:, TC:TC + 2], func=AF.Copy), reads=[zt], writes=[Tzcar[cb]])
            P.op("dve", lambda e: e.tensor_scalar(out=cacc.ap, in0=zt.ap[:, 2:TC + 2], scalar1=col(C_CW + 16 + cb), scalar2=col(C_CB + cb), op0=ALU.mult, op1=ALU.add),
                 reads=[zt, Tvec], writes=[cacc])
            P.op("dve", lambda e: e.scalar_tensor_tensor(out=cacc.ap, in0=zt.ap[:, 1:TC + 1], scalar=col(C_CW + 8 + cb), in1=cacc.ap, op0=ALU.mult, op1=ALU.add),
                 reads=[zt, Tvec, cacc], writes=[cacc])
            P.op("dve", lambda e: e.scalar_tensor_tensor(out=cacc.ap, in0=zt.ap[:, 0:TC], scalar=col(C_CW + cb), in1=cacc.ap, op0=ALU.mult, op1=ALU.add),
                 reads=[zt, Tvec, cacc], writes=[cacc])
            P.op("dve", lambda e: e.tensor_tensor(out=yb.ap[:, cb, :], in0=pb[6].ap[:], in1=cacc.ap, op=ALU.mult), reads=[pb[6], cacc], writes=[Tyb[cb]])
            if tc < NTC - 1:
                gs = slice(cb * 8, cb * 8 + 8)
                P.op("pe", lambda e: e.matmul(pb[7].ap[:, 0:8], perm[:], sm[:, I_LAST, gs], start=True, stop=True), reads=[Tperm, S_[I_LAST]], writes=[pb[7]])
                P.op("dve", lambda e: e.tensor_tensor(out=sm[:, I_T7, gs], in0=pb[7].ap[:, 0:8], in1=sm[:, I_SINL, gs], op=ALU.mult), reads=[pb[7], S_[I_SINL]], writes=[S_[I_T7]])
                P.op("dve", lambda e: e.tensor_tensor(out=sm[:, I_T6, gs], in0=sm[:, I_LAST, gs], in1=sm[:, I_COSL, gs], op=ALU.mult), reads=[S_[I_LAST], S_[I_COSL]], writes=[S_[I_T6]])
                P.op("dve", lambda e: e.tensor_tensor(out=sm[:, I_INIT, gs], in0=sm[:, I_T6, gs], in1=sm[:, I_T7, gs], op=ALU.add), reads=[S_[I_T6], S_[I_T7]], writes=[S_[I_INIT]])

        setup_batch(0)
        stageA_pe(0)
        stageA_conv(0)
        stageA_dve(0)
        stageA_add(0)
        for i in range(64):
            if i + 1 < 64:
                if (i + 1) % 8 == 0:
                    setup_batch((i + 1) // 8)
                stageA_pe(i + 1)
            if i >= 1:
                stageB_pe(i - 1)
            if i + 1 < 64:
                stageA_conv(i + 1)
            if i + 1 < 64:
                stageA_dve(i + 1)
            stageB_ew(i)
            if i + 1 < 64:
                stageA_add(i + 1)
            if i >= 1 and (i - 1) % 8 == 7:
                epilogue((i - 1) // 8)
        stageB_pe(63)
        epilogue(7)
        ph2 = Phase(base=base2)
        yg = ph2.alloc("yg", [128, 8, TC], BF16)
        Tyg = [T(yg.ap, f"yg{i}") for i in range(8)]
        for t in Tyg:
            t.r = dict(yg.r)
        gl_ = [ph2.alloc(f"gl{i}", [128, TC], F32) for i in range(2)]
        sa = [ph2.alloc(f"sa{i}", [128, TC], F32) for i in range(2)]
        sbb = [ph2.alloc(f"sbb{i}", [128, TC], F32) for i in range(2)]
        m1 = [ph2.alloc(f"m1{i}", [128, TC], F32) for i in range(2)]
        m2 = [ph2.alloc(f"m2{i}", [128, TC], F32) for i in range(2)]
        mg = ph2.alloc("merged", [128, 16, TC], BF16)
        Tmg = [T(mg.ap, f"mg{i}") for i in range(16)]
        for t in Tmg:
            t.r = dict(mg.r)
        mcnt = 0
        for (m0, n) in VSUB:
            gslot = ws.get()
            gv = ring_t[gslot][:, 0:8 * n * 128].rearrange("p (kc m) -> p kc m", kc=8)
            for q_ in range(n):
                m = m0 + q_
                bk = mcnt % 2
                for kc in range(8):
                    P.op("pe", lambda e, bk=bk, kc=kc, q_=q_, gv=gv: e.matmul(pb[bk].ap[:], gv[:, kc, q_ * 128:(q_ + 1) * 128], yabf.ap[:, kc, :], start=(kc == 0), stop=(kc == 7)),
                         reads=Tring[gslot] + Tya, writes=[pb[bk]])
                g_ = gl_[mcnt % 2]
                P.op("act", lambda e, g_=g_, bk=bk, m=m: e.activation(out=g_.ap, in_=pb[bk].ap[:], func=AF.Sigmoid, bias=col(C_BG + m)), reads=[pb[bk], Tvec], writes=[g_])
                P.op("dve", lambda e, g_=g_, m=m: e.tensor_tensor(out=yg.ap[:, m, :], in0=yabf.ap[:, m, :], in1=g_.ap, op=ALU.mult), reads=[Tya[m], g_], writes=[Tyg[m]])
                mcnt += 1
        for i in range(16):
            mslot = ws.get()
            r = ring_t[mslot]
            woc = r[:, 0:2048].rearrange("p (kc m) -> p kc m", kc=8)
            wa, wb = woc[:, :, 0:128], woc[:, :, 128:256]
            o = 0 if i % 2 == 0 else 4
            stored = i < NG
            if not stored:
                wgt = r[:, 2048:6144].rearrange("p (kc m) -> p kc m", kc=16)
                wga, wgb = wgt[:, :, 0:128], wgt[:, :, 128:256]
                for kc in range(16):
                    P.op("pe", lambda e, o=o, kc=kc, wga=wga: e.matmul(pb[o + 2].ap[:], wga[:, kc, :], uT_t[:, kc, :], start=(kc == 0), stop=(kc == 15)), reads=Tring[mslot] + TuT, writes=[pb[o + 2]])
                for kc in range(16):
                    P.op("pe", lambda e, o=o, kc=kc, wgb=wgb: e.matmul(pb[o + 3].ap[:], wgb[:, kc, :], uT_t[:, kc, :], start=(kc == 0), stop=(kc == 15)), reads=Tring[mslot] + TuT, writes=[pb[o + 3]])
            for kc in range(8):
                P.op("pe", lambda e, o=o, kc=kc, wa=wa: e.matmul(pb[o].ap[:], wa[:, kc, :], yg.ap[:, kc, :], start=(kc == 0), stop=(kc == 7)), reads=Tring[mslot] + Tyg, writes=[pb[o]])
            for kc in range(8):
                P.op("pe", lambda e, o=o, kc=kc, wb=wb: e.matmul(pb[o + 1].ap[:], wb[:, kc, :], yb.ap[:, kc, :], start=(kc == 0), stop=(kc == 7)), reads=Tring[mslot] + Tyb, writes=[pb[o + 1]])
            m1_, m2_ = m1[i % 2], m2[i % 2]
            if stored:
                sa_ap, sb_ap, gT = gsto[:, 2 * i, :], gsto[:, 2 * i + 1, :], [Tgsto[i]]
            else:
                sa_, sb_ = sa[i % 2], sbb[i % 2]
                P.op("act", lambda e, sa_=sa_, o=o: e.activation(out=sa_.ap, in_=pb[o + 2].ap[:], func=AF.Sigmoid), reads=[pb[o + 2]], writes=[sa_])
                P.op("act", lambda e, sb_=sb_, o=o: e.activation(out=sb_.ap, in_=pb[o + 3].ap[:], func=AF.Sigmoid), reads=[pb[o + 3]], writes=[sb_])
                sa_ap, sb_ap, gT = sa_.ap, sb_.ap, [sa_, sb_]
            P.op("dve", lambda e, m1_=m1_, sa_ap=sa_ap, o=o: e.tensor_tensor(out=m1_.ap, in0=pb[o].ap[:], in1=sa_ap, op=ALU.mult), reads=[pb[o]] + gT, writes=[m1_])
            P.op("dve", lambda e, m2_=m2_, sb_ap=sb_ap, o=o: e.tensor_tensor(out=m2_.ap, in0=pb[o + 1].ap[:], in1=sb_ap, op=ALU.mult), reads=[pb[o + 1]] + gT, writes=[m2_])
            P.op("dve", lambda e, m1_=m1_, m2_=m2_, i=i: e.tensor_tensor(out=mg.ap[:, i, :], in0=m1_.ap, in1=m2_.ap, op=ALU.add), reads=[m1_, m2_], writes=[Tmg[i]])
        for j in range(4):
            sA = ws.get()
            sB = ws.get()
            vA = ring_t[sA][:, 0:8 * 512].rearrange("p (fc c) -> p fc c", fc=8)
            vB = ring_t[sB][:, 0:8 * 512].rearrange("p (fc c) -> p fc c", fc=8)
            o = 0 if j % 2 == 0 else 4
            for i in range(4):
                for kc in range(16):
                    vw, sl = (vA, sA) if kc < 8 else (vB, sB)
                    P.op("pe", lambda e, o=o, i=i, kc=kc, vw=vw: e.matmul(pb[o + i].ap[:], mg.ap[:, kc, i * 128:(i + 1) * 128], vw[:, kc % 8, :], start=(kc == 0), stop=(kc == 15)),
                         reads=[Tmg[kc]] + Tring[sl], writes=[pb[o + i]])
                P.op("dve", lambda e, o=o, i=i, j=j: e.tensor_tensor(out=h_t[i][:, j * 512:(j + 1) * 512], in0=pb[o + i].ap[:], in1=h_t[i][:, j * 512:(j + 1) * 512], op=ALU.add),
                     reads=[pb[o + i], Th[i]], writes=[Th[i]])
        for parent, subs in ((yabf, Tya), (yb, Tyb), (yg, Tyg), (mg, Tmg)):
            for t in subs:
                if t.w is not None and parent.r.get(t.w[0], 0) < t.w[1]:
                    parent.r[t.w[0]] = t.w[1]
                for s_, v in t.r.items():
                    if parent.r.get(s_, 0) < v:
                        parent.r[s_] = v

    setup_gen = ssm_setup() if (do_mix or do_setup_only) else None
    xs = None
    for i in range(4):
        P.dma("sp", lambda e, i=i: e.dma_start(out=h_t[i][:], in_=x[i * 128:(i + 1) * 128, :]), writes=[Th[i]])
    pending_final = None
    for tc in range(NTC):
        if do_ffn1 and tc == 0:
            norm_T(C_G1)
            ffn(side_work=setup_gen)
        elif do_ffn1:
            norm_T(C_G1, src=xs)
            ffn(side_work=pending_final, src=xs, drain_first=True)
        elif tc == 0:
            if setup_gen is not None:
                for _ in setup_gen:
                    pass
        else:
            if pending_final is not None:
                for _ in pending_final:
                    pass
            for i in range(4):
                P.op("act", lambda e, i=i: e.activation(out=h_t[i][:], in_=xs[i].ap, func=AF.Copy), reads=[xs[i]], writes=[Th[i]])
        pending_final = None
        if do_mix:
            norm_T(C_GM)
            mixer(tc)
        if do_ffn2:
            norm_T(C_G2)
            if tc + 1 < NTC:
                xs = prefetch_x(tc + 1)
            ffn()
        elif tc + 1 < NTC:
            xs = prefetch_x(tc + 1)
        pending_final = final_gen(tc)
    for _ in pending_final:
        pass
    P.wait_all("sp", out_evs)
    print("NSEMS", len(P.sems), {e: len(P.q[e]) for e in P.ENG}, flush=True)
    P.emit()
    return nc


def prep_shared(inp):
    f32 = np.float32
    vec = np.zeros((128, 128), f32)

    def cols(v, n):
        return np.ascontiguousarray(np.asarray(v, f32).reshape(n, 128).T)

    vec[:, 0:16] = cols(inp["ffn1_norm"], 16)
    vec[:, 16:32] = cols(inp["mix_norm"], 16)
    vec[:, 32:48] = cols(inp["ffn2_norm"], 16)
    cw = np.asarray(inp["conv_w"], f32)
    for k in range(3):
        vec[:, 48 + 8 * k:56 + 8 * k] = cols(cw[k], 8)
    vec[:, 72:80] = cols(inp["conv_b"], 8)
    vec[:, 80:88] = cols(inp["ssm_d"], 8)
    vec[:, 88:96] = cols(inp["ssm_b_glu"], 8)
    vec[0:64, 96] = 1.0
    vec[64:128, 96] = -1.0
    vec[0:64, 97] = -1.0
    vec[64:128, 97] = 1.0
    vec[0:64, 98] = 2 * np.pi
    vec[64:128, 98] = -2 * np.pi
    vec[0:64, 99] = -2 * np.pi
    vec[64:128, 99] = 2 * np.pi
    gfin = np.ascontiguousarray(np.broadcast_to(np.asarray(inp["final_norm"], f32), (128, D)))
    lam = np.zeros((128, 3, 64), f32)
    lre = np.asarray(inp["ssm_lambda_re"], f32).T
    lim = np.asarray(inp["ssm_lambda_im"], f32).T
    lam[0:64, 0], lam[64:128, 0] = lre, lre
    lam[0:64, 1], lam[64:128, 1] = lim, lim
    lam[:, 2, :] = np.asarray(inp["ssm_log_dt"], f32)[None, :]
    bre = np.asarray(inp["ssm_b_re"], f32)
    bim = np.asarray(inp["ssm_b_im"], f32)
    cre = np.asarray(inp["ssm_c_re"], f32)
    cim = np.asarray(inp["ssm_c_im"], f32)
    BT = np.zeros((128, 64, 2, 128), f32)
    CD = np.zeros((128, 64, 2, 128), f32)
    for g in range(64):
        r0 = (g % 8) * 16
        BT[r0:r0 + 16, g, 0, 0:64] = bre[g].T
        BT[r0:r0 + 16, g, 0, 64:128] = bim[g].T
        BT[r0:r0 + 16, g, 1, 0:64] = bim[g].T
        BT[r0:r0 + 16, g, 1, 64:128] = bre[g].T
        CD[0:64, g, 0, r0:r0 + 16] = cre[g].T
        CD[64:128, g, 0, r0:r0 + 16] = cre[g].T
        CD[0:64, g, 1, r0:r0 + 16] = cim[g].T
        CD[64:128, g, 1, r0:r0 + 16] = cim[g].T
    perm = np.zeros((128, 128), f32)
    for n in range(64):
        perm[64 + n, n] = -1.0
        perm[n, 64 + n] = 1.0
    w_in_h = np.asarray(inp["w_in"], f32)
    w_cv = np.ascontiguousarray(w_in_h[:, 1024:4096].reshape(D, 3, 8, 128).transpose(0, 2, 1, 3).reshape(D, 3072))
    w_gate = np.ascontiguousarray(w_in_h[:, 4096:8192].reshape(D, 2, 16, 128).transpose(0, 2, 1, 3).reshape(D, 4096))
    w_oc = np.ascontiguousarray(np.stack([np.asarray(inp["ssm_w_out"], f32).reshape(W, 16, 128),
                                          np.asarray(inp["conv_w_out"], f32).reshape(W, 16, 128)], axis=2).reshape(W, 4096))
    sh = dict(
        w_cv=w_cv, w_gate=w_gate, w_oc=w_oc,
        w1g=np.asarray(inp["ffn1_w_gate"], f32), w1u=np.asarray(inp["ffn1_w_up"], f32), w1d=np.asarray(inp["ffn1_w_down"], f32),
        w2g=np.asarray(inp["ffn2_w_gate"], f32), w2u=np.asarray(inp["ffn2_w_up"], f32), w2d=np.asarray(inp["ffn2_w_down"], f32),
        w_in=np.asarray(inp["w_in"], f32), w_glu=np.asarray(inp["ssm_w_glu"], f32), w_o=np.asarray(inp["w_o"], f32),
        vec=vec, gfin=gfin, lam=lam, BT=BT, CD=CD, ident=np.eye(128, dtype=f32), perm=perm,
        iota=np.ascontiguousarray(np.broadcast_to(np.arange(TC, dtype=f32), (128, TC))),
    )
    return sh


_NC_CACHE = {}


def run(inputs, stage="full", cores=8):
    if stage not in _NC_CACHE:
        _NC_CACHE[stage] = build(stage)
    nc = _NC_CACHE[stage]
    sh = prep_shared(inputs)
    xs = np.asarray(inputs["x"], np.float32)
    in_maps = []
    for c in range(cores):
        m = dict(sh)
        m["x"] = np.ascontiguousarray(xs[c])
        in_maps.append(m)
    res = run_bass_kernel_spmd(nc, in_maps, core_ids=list(range(cores)))
    return np.stack([np.asarray(r["out"]) for r in res.results], 0)


def kernel(**inputs):
    return run(inputs, "full", 8).astype(np.float32)
```

```python
import numpy as np
import concourse.bass as bass
import concourse.mybir as mybir
from concourse.bass_utils import run_bass_kernel_spmd

F32 = mybir.dt.float32
BF16 = mybir.dt.bfloat16
I32 = mybir.dt.int32
AF = mybir.ActivationFunctionType
ALU = mybir.AluOpType

D = 2048
S = 2048
FF = 5504
NF = FF // 128
TC = 512
NTC = S // TC
W = 1024
G = 64
NS = 64
TWO_PI = float(2 * np.pi)


class T:
    __slots__ = ("ap", "w", "r", "name", "dsem", "dcnt", "off", "size")

    def __init__(self, ap, name=""):
        self.ap = ap
        self.w = None
        self.r = {}
        self.name = name
        self.dsem = None
        self.dcnt = 0
        self.off = None
        self.size = None


class Prog:
    ENG = ("pe", "act", "dve", "pool", "sp")
    EPOCH = 30000

    def __init__(self, nc):
        self.nc = nc
        self.q = {e: [] for e in self.ENG}
        self.seen = {e: {} for e in self.ENG}
        self.sems = []
        self.esem = {e: None for e in self.ENG}
        self.ecnt = {e: 0 for e in self.ENG}
        self.dsems = {}

    def newsem(self, name):
        h = self.nc.alloc_semaphore(name=name)
        self.sems.append(h)
        return len(self.sems) - 1

    def _engev(self, e):
        if self.esem[e] is None or self.ecnt[e] >= self.EPOCH:
            self.esem[e] = self.newsem(f"s_{e}_{len(self.sems)}")
            self.ecnt[e] = 0
        self.ecnt[e] += 1
        return (self.esem[e], self.ecnt[e])

    def _waits(self, eng, reads, writes):
        evs = {}
        for t in reads:
            if t.w is not None and evs.get(t.w[0], 0) < t.w[1]:
                evs[t.w[0]] = t.w[1]
        for t in writes:
            if t.w is not None and evs.get(t.w[0], 0) < t.w[1]:
                evs[t.w[0]] = t.w[1]
            for s, v in t.r.items():
                if evs.get(s, 0) < v:
                    evs[s] = v
        waits = []
        seen = self.seen[eng]
        for s, v in evs.items():
            if eng == "pe" and s == self.esem["pe"]:
                continue
            if seen.get(s, 0) >= v:
                continue
            seen[s] = v
            waits.append((s, v))
        return waits

    def _commit(self, ev, reads, writes):
        s, v = ev
        for t in writes:
            t.w = ev
            t.r = {}
        for t in reads:
            if t.r.get(s, 0) < v:
                t.r[s] = v

    def op(self, eng, fn, reads=(), writes=()):
        waits = self._waits(eng, reads, writes)
        ev = self._engev(eng)
        self.q[eng].append((waits, fn, (ev[0], 1)))
        self._commit(ev, reads, writes)
        return ev

    def dma(self, eng, fn, reads=(), writes=(), sem_tile=None):
        waits = self._waits(eng, reads, writes)
        st = sem_tile if sem_tile is not None else (writes[0] if writes else reads[0])
        if st.name not in self.dsems:
            self.dsems[st.name] = [self.newsem(f"d_{st.name}"), 0]
        rec = self.dsems[st.name]
        rec[1] += 16
        ev = (rec[0], rec[1])
        self.q[eng].append((waits, fn, (rec[0], 16)))
        self._commit(ev, reads, writes)
        return ev

    def wait_all(self, eng, evs):
        waits = []
        seen = self.seen[eng]
        for s, v in evs:
            if seen.get(s, 0) >= v:
                continue
            seen[s] = v
            waits.append((s, v))
        self.q[eng].append((waits, None, None))

    def emit(self):
        nc = self.nc
        sems = self.sems
        with nc.Block() as block:
            def mk(e):
                def body(eng):
                    for waits, fn, inc in self.q[e]:
                        for s, v in waits:
                            eng.wait_ge(sems[s], v)
                        if fn is not None:
                            ins = fn(eng)
                            if inc is not None:
                                ins.then_inc(sems[inc[0]], inc[1])
                return body
            block.tensor(mk("pe"))
            block.scalar(mk("act"))
            block.vector(mk("dve"))
            block.gpsimd(mk("pool"))
            block.sync(mk("sp"))


def build(stage="full"):
    nc = bass.Bass("TRN2", target_bir_lowering=False)
    P = Prog(nc)

    def din(name, shape, dt=F32):
        return nc.dram_tensor(name, list(shape), dt, kind="ExternalInput").ap()

    x = din("x", [S, D])
    w1g = din("w1g", [D, FF]); w1u = din("w1u", [D, FF]); w1d = din("w1d", [FF, D])
    w2g = din("w2g", [D, FF]); w2u = din("w2u", [D, FF]); w2d = din("w2d", [FF, D])
    w_in = din("w_in", [D, 8192])
    w_glu = din("w_glu", [W, W])
    w_o = din("w_o", [D, D])
    w_cv = din("w_cv", [D, 3072])
    w_gate = din("w_gate", [D, 4096])
    w_oc = din("w_oc", [W, 4096])
    vec_d = din("vec", [128, 128])
    gfin_d = din("gfin", [128, D])
    lam_d = din("lam", [128, 3, 64])
    BT_d = din("BT", [128, 64, 2, 128])
    CD_d = din("CD", [128, 64, 2, 128])
    ident_d = din("ident", [128, 128])
    perm_d = din("perm", [128, 128])
    iota_d = din("iota", [128, TC])
    out = nc.dram_tensor("out", [S, D], F32, kind="ExternalOutput").ap()
    Cs_d = nc.dram_tensor("Cs_scr", [128, 64, 2, 128], BF16, kind="Internal").ap()
    TAB_d = nc.dram_tensor("TAB_scr", [128, 64, 2, TC], F32, kind="Internal").ap()
    _tcs = [T(None, f"Cs_d{b}") for b in range(2)]
    T_Cs = [_tcs[b // 4] for b in range(8)]
    _ttab = [T(None, f"TAB_d{g}") for g in range(4)]
    T_TAB = [_ttab[gp // 8] for gp in range(32)]

    def sb(name, shape, dt):
        return nc.sbuf_tensor(name, list(shape), dt).__enter__()

    pb = []
    for i in range(8):
        t = nc.psum_tensor(f"pb{i}", [128, 512], F32).__enter__()
        pb.append(T(t, f"pb{i}"))

    def pbf(i):
        return pb[i].ap[:].bitcast(BF16)

    h_t = [sb(f"h{i}", [128, D], F32) for i in range(4)]
    Th = [T(h_t[i], f"h{i}") for i in range(4)]
    uT_t = sb("uT", [128, 16, TC], BF16)
    TuT = [T(uT_t, f"uT{i}") for i in range(4)]
    xn_t = [sb(f"xn{i}", [128, D], BF16) for i in range(2)]
    Txn = [T(xn_t[i], f"xn{i}") for i in range(2)]
    NSLOT = 5
    SLOT = 6144
    ring_t = [sb(f"ring{i}", [128, SLOT], BF16) for i in range(NSLOT)]
    Tring = [[T(ring_t[i], f"ring{i}_{r}") for r in range(4)] for i in range(NSLOT)]
    for tl_ in Tring:
        for t_ in tl_:
            P.dsems[t_.name] = [P.newsem(f"d_{t_.name}"), 0]
    vec = sb("vecs", [128, 128], F32); Tvec = T(vec, "vec")
    ident_f = sb("ident_f", [128, 128], F32); Tidf = T(ident_f, "idf")
    ident_b = sb("ident_b", [128, 128], BF16); Tidb = T(ident_b, "idb")
    perm = sb("perm_s", [128, 128], F32); Tperm = T(perm, "perm")
    iota = sb("iota_s", [128, TC], F32); Tiota = T(iota, "iota")
    sm = sb("small", [128, 24, 64], F32)
    Tsm = [T(sm, f"sm{i}") for i in range(24)]
    stat = sb("stat", [128, 16], F32); Tstat = [T(stat, f"stat{i}") for i in range(4)]
    cst = sb("cst", [128, 4], F32); Tcst = T(cst, "cst")
    zcar = sb("zcar", [128, 8, 2], F32); Tzcar = [T(zcar, f"zcar{i}") for i in range(8)]
    ARENA = 74 * 1024 // 2
    arena = sb("arena", [128, ARENA], BF16)

    C_G1, C_GM, C_G2, C_CW, C_CB, C_SD, C_BG, C_SGN, C_NSGN, C_SC2PI, C_NSC2PI = 0, 16, 32, 48, 72, 80, 88, 96, 97, 98, 99
    (I_LRE, I_LIM, I_LDT, I_DT, I_TH, I_R, I_T0, I_T1, I_T2, I_T3, I_FRE, I_FIM, I_FA, I_FB, I_FA2, I_FB2,
     I_COSL, I_SINL, I_LAST, I_INIT, I_T4, I_T5, I_T6, I_T7) = range(24)

    arena_tiles = []

    class Phase:
        def __init__(self, base=0):
            self.ptr = base

        def alloc(self, name, shape, dt):
            n = 1
            for s_ in shape[1:]:
                n *= s_
            nbytes = n * (2 if dt == BF16 else 4)
            nb16 = (nbytes + 63) // 64 * 32
            off = self.ptr
            self.ptr += nb16
            assert self.ptr <= ARENA, (name, self.ptr, ARENA)
            ap = arena[:, off:off + nbytes // 2]
            if dt != BF16:
                ap = ap.bitcast(dt)
            if len(shape) == 3:
                ap = ap.rearrange("p (a b) -> p a b", a=shape[1])
            elif len(shape) == 4:
                ap = ap.rearrange("p (a b c) -> p a b c", a=shape[1], b=shape[2])
            t = T(ap, name)
            t.off, t.size = off, nb16
            for o in arena_tiles:
                if o.off < off + nb16 and off < o.off + o.size:
                    if o.w is not None and t.r.get(o.w[0], 0) < o.w[1]:
                        t.r[o.w[0]] = o.w[1]
                    for s_, v in o.r.items():
                        if t.r.get(s_, 0) < v:
                            t.r[s_] = v
            arena_tiles.append(t)
            return t

    class WS:
        def __init__(self):
            self.loads = []
            self.wts = []
            self.issued = 0
            self.cur = -1

        def add(self, fn, wt=128):
            self.loads.append(fn)
            self.wts.append(wt)

        def get(self):
            self.cur += 1
            k = self.cur
            while self.issued < len(self.loads) and self.issued <= k + NSLOT - 2:
                if self.issued > k and sum(self.wts[k:self.issued + 1]) > 1200:
                    break
                self.loads[self.issued](self.issued % NSLOT)
                self.issued += 1
            return k % NSLOT

    ws = WS()

    def wload(slot, dst_ap, src_ap, sub=None):
        if sub is None:
            P.dma("pool", lambda e: e.dma_start(out=dst_ap, in_=src_ap, max_dma_last_dim=2048), writes=Tring[slot], sem_tile=Tring[slot][0])
        else:
            P.dma("pool", lambda e: e.dma_start(out=dst_ap, in_=src_ap, max_dma_last_dim=2048), writes=[Tring[slot][sub]])

    def ld_A(wd, m0, mw, KC=16):
        def f(slot):
            dst = ring_t[slot][:, 0:KC * mw].rearrange("p (kc m) -> p kc m", kc=KC)
            src = wd.rearrange("(kc p) m -> p kc m", p=128)[:, :, m0:m0 + mw]
            wload(slot, dst, src)
        return f

    def ld_B(wd, f0, nfc, c0, cw):
        def f(slot):
            dst = ring_t[slot][:, 0:nfc * cw].rearrange("p (fc c) -> p fc c", fc=nfc)
            src = wd[f0 * 128:(f0 + nfc) * 128, :].rearrange("(fc p) c -> p fc c", p=128)[:, :, c0:c0 + cw]
            wload(slot, dst, src)
        return f

    FBLK = [(0, 12), (12, 12), (24, 12), (36, 7)]

    def ffn_loads(wg, wu, wd):
        for (f0, nf) in FBLK:
            sub = []
            k = 0
            while k < nf:
                n = min(3, nf - k)
                sub.append((f0 + k, n))
                k += n
            for (fs, n) in sub:
                ws.add(ld_A(wg, fs * 128, n * 128))
                ws.add(ld_A(wu, fs * 128, n * 128))
            for j in range(4):
                ws.add(ld_B(wd, f0, nf, j * 512, 512))

    VSUB = [(0, 3), (3, 3), (6, 2)]

    def ld_bt(cb):
        def f(slot):
            dst = ring_t[slot][:, 0:8 * 2 * 128].rearrange("p (g v n) -> p g v n", g=8, v=2)
            wload(slot, dst, BT_d[:, cb * 8:(cb + 1) * 8, :, :])
        return f

    def ld_conv(cb):
        def f(slot):
            dst = ring_t[slot][:, 0:16 * 384].rearrange("p (kc m) -> p kc m", kc=16)
            wload(slot, dst, w_cv.rearrange("(kc p) m -> p kc m", p=128)[:, :, cb * 384:(cb + 1) * 384])
        return f

    def ld_m5(i):
        def f(slot):
            r = ring_t[slot]
            d0 = r[:, 0:8 * 256].rearrange("p (kc m) -> p kc m", kc=8)
            wload(slot, d0, w_oc.rearrange("(kc p) m -> p kc m", p=128)[:, :, i * 256:(i + 1) * 256], sub=0)
            d1 = r[:, 2048:2048 + 16 * 256].rearrange("p (kc m) -> p kc m", kc=16)
            wload(slot, d1, w_gate.rearrange("(kc p) m -> p kc m", p=128)[:, :, i * 256:(i + 1) * 256], sub=1)
        return f

    def mixer_loads():
        for cb in range(8):
            ws.add(ld_A(w_in, cb * 128, 128))
            ws.add(ld_bt(cb), 32)
            ws.add(ld_conv(cb), 128)
        for (m0, n) in VSUB:
            ws.add(ld_A(w_glu, m0 * 128, n * 128, KC=8), 64)
        for i in range(16):
            ws.add(ld_m5(i), 192)
        for j in range(4):
            ws.add(ld_B(w_o, 0, 8, j * 512, 512), 64)
            ws.add(ld_B(w_o, 8, 8, j * 512, 512), 64)

    do_ffn1 = stage in ("full", "ffn1", "ffn1mix")
    do_mix = stage in ("full", "mix", "ffn1mix")
    do_setup_only = stage == "setup"
    do_ffn2 = stage in ("full",)
    for tc in range(NTC):
        if do_ffn1:
            ffn_loads(w1g, w1u, w1d)
        if do_mix:
            mixer_loads()
        if do_ffn2:
            ffn_loads(w2g, w2u, w2d)

    P.dma("sp", lambda e: e.dma_start(out=vec[:], in_=vec_d), writes=[Tvec])
    P.dma("sp", lambda e: e.dma_start(out=ident_f[:], in_=ident_d), writes=[Tidf])
    P.op("act", lambda e: e.activation(out=ident_b[:], in_=ident_f[:], func=AF.Copy), reads=[Tidf], writes=[Tidb])
    P.op("dve", lambda e: e.memset(cst[:, 0:1], 1e-6), writes=[Tcst])
    P.op("dve", lambda e: e.memset(cst[:, 1:2], 0.0), writes=[Tcst])
    P.op("dve", lambda e: e.memset(cst[:, 2:3], 8.25), writes=[Tcst])

    def col(c):
        return vec[:, c:c + 1]

    def prefetch_x(tc):
        ph = Phase(base=8192)
        xs = [ph.alloc(f"xs{i}", [128, D], F32) for i in range(4)]
        for i in range(4):
            r0 = tc * TC + i * 128
            P.dma("sp", lambda e, i=i, r0=r0: e.dma_start(out=xs[i].ap, in_=x[r0:r0 + 128, :]), writes=[xs[i]])
        return xs

    def rstd_for(i):
        j = i % 2
        P.op("act", lambda e: e.activation(out=xn_t[j][:], in_=h_t[i][:], func=AF.Square, accum_out=stat[:, 4 * i:4 * i + 1]),
             reads=[Th[i]], writes=[Txn[j], Tstat[i]])
        P.op("act", lambda e: e.activation(out=stat[:, 4 * i + 1:4 * i + 2], in_=stat[:, 4 * i:4 * i + 1], func=AF.Sqrt,
                                           scale=1.0 / D, bias=cst[:, 0:1]),
             reads=[Tstat[i], Tcst], writes=[Tstat[i]])
        P.op("dve", lambda e: e.reciprocal(out=stat[:, 4 * i + 2:4 * i + 3], in_=stat[:, 4 * i + 1:4 * i + 2]),
             reads=[Tstat[i]], writes=[Tstat[i]])

    def norm_T(gcol0, src=None):
        ph = Phase(base=0)
        xq = [ph.alloc(f"xq{i}", [128, D], BF16) for i in range(4)]
        if src is None:
            sap = [h_t[i][:] for i in range(4)]
            sT = Th
        else:
            sap = [t.ap for t in src]
            sT = src
        for i in range(4):
            if i < 2:
                P.op("dve", lambda e, i=i: e.scalar_tensor_tensor(out=xq[i].ap, in0=sap[i], scalar=1.0, in1=sap[i], op0=ALU.mult, op1=ALU.mult,
                                                                 accum_out=stat[:, 4 * i:4 * i + 1]),
                     reads=[sT[i]], writes=[xq[i], Tstat[i]])
            else:
                P.op("act", lambda e, i=i: e.activation(out=xq[i].ap, in_=sap[i], func=AF.Square, accum_out=stat[:, 4 * i:4 * i + 1]),
                     reads=[sT[i]], writes=[xq[i], Tstat[i]])
        for i in range(4):
            P.op("act", lambda e, i=i: e.activation(out=stat[:, 4 * i + 1:4 * i + 2], in_=stat[:, 4 * i:4 * i + 1], func=AF.Sqrt,
                                                    scale=1.0 / D, bias=cst[:, 0:1]),
                 reads=[Tstat[i], Tcst], writes=[Tstat[i]])
            P.op("dve", lambda e, i=i: e.reciprocal(out=stat[:, 4 * i + 2:4 * i + 3], in_=stat[:, 4 * i + 1:4 * i + 2]),
                 reads=[Tstat[i]], writes=[Tstat[i]])
        for i in range(4):
            P.op("act", lambda e, i=i: e.activation(out=xq[i].ap, in_=sap[i], func=AF.Copy, scale=stat[:, 4 * i + 2:4 * i + 3]),
                 reads=[sT[i], Tstat[i]], writes=[xq[i]])
            for half in range(2):
                bank = 6 + half
                pv = pbf(bank)
                for k in range(8):
                    kc = half * 8 + k
                    P.op("pe", lambda e, pv=pv, k=k, kc=kc, i=i: e.transpose(pv[:, k * 128:(k + 1) * 128], xq[i].ap[:, kc * 128:(kc + 1) * 128], ident_b[:]),
                         reads=[xq[i], Tidb], writes=[pb[bank]])
                g0 = gcol0 + half * 8
                P.op("dve", lambda e, pv=pv, half=half, i=i, g0=g0: e.tensor_tensor(
                        out=uT_t[:, half * 8:(half + 1) * 8, i * 128:(i + 1) * 128],
                        in0=pv.rearrange("p (k m) -> p k m", k=8),
                        in1=vec[:, g0:g0 + 8].unsqueeze(2).broadcast_to([128, 8, 128]), op=ALU.mult),
                     reads=[pb[bank], Tvec], writes=[TuT[i]])

    def ffn(side_work=None, src=None, drain_first=False):
        ph = Phase()
        first_blk = True
        act = ph.alloc("act", [128, 12, TC], BF16)
        sg = [ph.alloc(f"sg{i}", [128, TC], F32) for i in range(2)]
        Tact = [T(act.ap, f"act{i}") for i in range(12)]
        for t in Tact:
            t.r = dict(act.r)
        cnt = 0
        dn = 0
        for (f0, nf) in FBLK:
            k = 0
            while k < nf:
                n = min(3, nf - k)
                sg_slot = ws.get()
                su_slot = ws.get()
                wgv = ring_t[sg_slot][:, 0:16 * n * 128].rearrange("p (kc m) -> p kc m", kc=16)
                wuv = ring_t[su_slot][:, 0:16 * n * 128].rearrange("p (kc m) -> p kc m", kc=16)
                for q in range(n):
                    fl = k + q
                    gb_, ub_ = cnt % 2, 2 + cnt % 2
                    for kc in range(16):
                        P.op("pe", lambda e, gb_=gb_, wgv=wgv, q=q, kc=kc: e.matmul(pb[gb_].ap[:], wgv[:, kc, q * 128:(q + 1) * 128], uT_t[:, kc, :], start=(kc == 0), stop=(kc == 15)),
                             reads=Tring[sg_slot] + TuT, writes=[pb[gb_]])
                    for kc in range(16):
                        P.op("pe", lambda e, ub_=ub_, wuv=wuv, q=q, kc=kc: e.matmul(pb[ub_].ap[:], wuv[:, kc, q * 128:(q + 1) * 128], uT_t[:, kc, :], start=(kc == 0), stop=(kc == 15)),
                             reads=Tring[su_slot] + TuT, writes=[pb[ub_]])
                    s_ = sg[cnt % 2]
                    P.op("act", lambda e, s_=s_, gb_=gb_: e.activation(out=s_.ap, in_=pb[gb_].ap[:], func=AF.Silu), reads=[pb[gb_]], writes=[s_])
                    P.op("dve", lambda e, s_=s_, ub_=ub_, fl=fl: e.tensor_tensor(out=act.ap[:, fl, :], in0=pb[ub_].ap[:], in1=s_.ap, op=ALU.mult),
                         reads=[pb[ub_], s_], writes=[Tact[fl]])
                    cnt += 1
                    for _ in range(3):
                        if side_work is not None:
                            try:
                                next(side_work)
                            except StopIteration:
                                side_work = None
                k += n
            if first_blk and side_work is not None and drain_first:
                for _ in side_work:
                    pass
                side_work = None
            for j in range(4):
                sd_slot = ws.get()
                wdv = ring_t[sd_slot][:, 0:nf * 512].rearrange("p (fc c) -> p fc c", fc=nf)
                for i in range(4):
                    db = 4 + dn % 2
                    dn += 1
                    for fl in range(nf):
                        P.op("pe", lambda e, db=db, fl=fl, i=i, wdv=wdv, nf=nf: e.matmul(pb[db].ap[:], act.ap[:, fl, i * 128:(i + 1) * 128], wdv[:, fl, :], start=(fl == 0), stop=(fl == nf - 1)),
                             reads=[Tact[fl]] + Tring[sd_slot], writes=[pb[db]])
                    if first_blk and src is not None:
                        rs_ap, rs_T = src[i].ap[:, j * 512:(j + 1) * 512], src[i]
                    else:
                        rs_ap, rs_T = h_t[i][:, j * 512:(j + 1) * 512], Th[i]
                    P.op("dve", lambda e, db=db, i=i, j=j, rs_ap=rs_ap: e.scalar_tensor_tensor(out=h_t[i][:, j * 512:(j + 1) * 512], in0=pb[db].ap[:], scalar=0.5,
                                                                             in1=rs_ap, op0=ALU.mult, op1=ALU.add),
                         reads=[pb[db], rs_T], writes=[Th[i]])
            first_blk = False
        for t in Tact:
            if t.w is not None and act.r.get(t.w[0], 0) < t.w[1]:
                act.r[t.w[0]] = t.w[1]
            for s_, v in t.r.items():
                if act.r.get(s_, 0) < v:
                    act.r[s_] = v
        if side_work is not None:
            for _ in side_work:
                pass

    out_evs = []

    def final_gen(tc):
        ph = Phase(base=24576)
        gf = ph.alloc("gfin", [128, D], F32)
        ob = [ph.alloc(f"ob{i}", [128, D], F32) for i in range(2)]
        P.dma("sp", lambda e: e.dma_start(out=gf.ap, in_=gfin_d), writes=[gf])
        for i in range(4):
            rstd_for(i)
            yield
            o = ob[i % 2]
            P.op("dve", lambda e, i=i, o=o: e.scalar_tensor_tensor(out=o.ap, in0=h_t[i][:], scalar=stat[:, 4 * i + 2:4 * i + 3], in1=gf.ap,
                                                                  op0=ALU.mult, op1=ALU.mult),
                 reads=[Th[i], Tstat[i], gf], writes=[o])
            r0 = tc * TC + i * 128
            ev = P.dma("sp", lambda e, o=o, r0=r0: e.dma_start(out=out[r0:r0 + 128, :], in_=o.ap), reads=[o], sem_tile=o)
            out_evs.append(ev)
            yield

    def smv(i):
        return sm[:, i, :]

    def ssm_setup():
        S_ = Tsm
        P.dma("sp", lambda e: e.dma_start(out=sm[:, 0:3, :], in_=lam_d), writes=[S_[I_LRE], S_[I_LIM], S_[I_LDT]])
        P.dma("sp", lambda e: e.dma_start(out=perm[:], in_=perm_d), writes=[Tperm])
        P.dma("sp", lambda e: e.dma_start(out=iota[:], in_=iota_d), writes=[Tiota])
        P.op("dve", lambda e: e.memset(smv(I_INIT), 0.0), writes=[S_[I_INIT]])
        P.op("dve", lambda e: e.memset(zcar[:].rearrange("p a b -> p (a b)"), 0.0), writes=Tzcar)
        P.op("act", lambda e: e.activation(out=smv(I_DT), in_=smv(I_LDT), func=AF.Exp), reads=[S_[I_LDT]], writes=[S_[I_DT]])
        P.op("dve", lambda e: e.tensor_scalar(out=smv(I_LRE), in0=smv(I_LRE), scalar1=-1e-4, scalar2=None, op0=ALU.min), reads=[S_[I_LRE]], writes=[S_[I_LRE]])
        P.op("dve", lambda e: e.tensor_tensor(out=smv(I_T0), in0=smv(I_LRE), in1=smv(I_DT), op=ALU.mult), reads=[S_[I_LRE], S_[I_DT]], writes=[S_[I_T0]])
        P.op("act", lambda e: e.activation(out=smv(I_R), in_=smv(I_T0), func=AF.Exp), reads=[S_[I_T0]], writes=[S_[I_R]])
        P.op("dve", lambda e: e.scalar_tensor_tensor(out=smv(I_TH), in0=smv(I_LIM), scalar=1.0 / TWO_PI, in1=smv(I_DT), op0=ALU.mult, op1=ALU.mult),
             reads=[S_[I_LIM], S_[I_DT]], writes=[S_[I_TH]])
        yield

        def trig(dst_i, src_i, mul, add, scale_ap=None):
            P.op("dve", lambda e: e.tensor_scalar(out=smv(I_T4), in0=smv(src_i), scalar1=float(mul), scalar2=float(add), op0=ALU.mult, op1=ALU.add),
                 reads=[S_[src_i]], writes=[S_[I_T4]])
            reduce_turns(smv(I_T4), smv(I_T5).bitcast(I32), smv(I_T6), [S_[I_T4]], [S_[I_T5]], [S_[I_T6]])
            P.op("act", lambda e: e.activation(out=smv(dst_i), in_=smv(I_T4), func=AF.Sin, scale=TWO_PI), reads=[S_[I_T4]], writes=[S_[dst_i]])

        trig(I_T1, I_TH, 1.0, 8.25)
        trig(I_T2, I_TH, 1.0, 8.0)
        trig(I_COSL, I_TH, float(TC), 8.25)
        trig(I_SINL, I_TH, float(TC), 8.0)
        yield
        P.op("dve", lambda e: e.tensor_tensor(out=smv(I_T1), in0=smv(I_T1), in1=smv(I_R), op=ALU.mult), reads=[S_[I_T1], S_[I_R]], writes=[S_[I_T1]])
        P.op("dve", lambda e: e.tensor_tensor(out=smv(I_T2), in0=smv(I_T2), in1=smv(I_R), op=ALU.mult), reads=[S_[I_T2], S_[I_R]], writes=[S_[I_T2]])
        P.op("dve", lambda e: e.tensor_scalar(out=smv(I_T1), in0=smv(I_T1), scalar1=-1.0, scalar2=None, op0=ALU.add), reads=[S_[I_T1]], writes=[S_[I_T1]])
        P.op("dve", lambda e: e.tensor_tensor(out=smv(I_T3), in0=smv(I_LRE), in1=smv(I_LRE), op=ALU.mult), reads=[S_[I_LRE]], writes=[S_[I_T3]])
        P.op("dve", lambda e: e.tensor_tensor(out=smv(I_T4), in0=smv(I_LIM), in1=smv(I_LIM), op=ALU.mult), reads=[S_[I_LIM]], writes=[S_[I_T4]])
        P.op("dve", lambda e: e.tensor_tensor(out=smv(I_T3), in0=smv(I_T3), in1=smv(I_T4), op=ALU.add), reads=[S_[I_T3], S_[I_T4]], writes=[S_[I_T3]])
        P.op("dve", lambda e: e.reciprocal(out=smv(I_T3), in_=smv(I_T3)), reads=[S_[I_T3]], writes=[S_[I_T3]])
        P.op("dve", lambda e: e.tensor_tensor(out=smv(I_T4), in0=smv(I_T1), in1=smv(I_LRE), op=ALU.mult), reads=[S_[I_T1], S_[I_LRE]], writes=[S_[I_T4]])
        P.op("dve", lambda e: e.tensor_tensor(out=smv(I_T5), in0=smv(I_T2), in1=smv(I_LIM), op=ALU.mult), reads=[S_[I_T2], S_[I_LIM]], writes=[S_[I_T5]])
        P.op("dve", lambda e: e.tensor_tensor(out=smv(I_T4), in0=smv(I_T4), in1=smv(I_T5), op=ALU.add), reads=[S_[I_T4], S_[I_T5]], writes=[S_[I_T4]])
        P.op("dve", lambda e: e.tensor_tensor(out=smv(I_FRE), in0=smv(I_T4), in1=smv(I_T3), op=ALU.mult), reads=[S_[I_T4], S_[I_T3]], writes=[S_[I_FRE]])
        P.op("dve", lambda e: e.tensor_tensor(out=smv(I_T4), in0=smv(I_T2), in1=smv(I_LRE), op=ALU.mult), reads=[S_[I_T2], S_[I_LRE]], writes=[S_[I_T4]])
        P.op("dve", lambda e: e.tensor_tensor(out=smv(I_T5), in0=smv(I_T1), in1=smv(I_LIM), op=ALU.mult), reads=[S_[I_T1], S_[I_LIM]], writes=[S_[I_T5]])
        P.op("dve", lambda e: e.tensor_tensor(out=smv(I_T4), in0=smv(I_T4), in1=smv(I_T5), op=ALU.subtract), reads=[S_[I_T4], S_[I_T5]], writes=[S_[I_T4]])
        P.op("dve", lambda e: e.tensor_tensor(out=smv(I_FIM), in0=smv(I_T4), in1=smv(I_T3), op=ALU.mult), reads=[S_[I_T4], S_[I_T3]], writes=[S_[I_FIM]])
        lo, hi = slice(0, 64), slice(64, 128)
        rd = [S_[I_FRE], S_[I_FIM]]
        P.op("dve", lambda e: e.tensor_scalar(out=smv(I_T6), in0=smv(I_FIM), scalar1=-1.0, scalar2=None, op0=ALU.mult), reads=rd, writes=[S_[I_T6]])
        P.op("dve", lambda e: e.tensor_scalar(out=smv(I_T7), in0=smv(I_FRE), scalar1=-1.0, scalar2=None, op0=ALU.mult), reads=rd, writes=[S_[I_T7]])
        rd2 = rd + [S_[I_T6], S_[I_T7]]
        P.op("dve", lambda e: e.tensor_copy(out=sm[lo, I_FA, :], in_=sm[lo, I_FRE, :]), reads=rd2, writes=[S_[I_FA]])
        P.op("dve", lambda e: e.tensor_copy(out=sm[hi, I_FA, :], in_=sm[hi, I_T6, :]), reads=rd2, writes=[S_[I_FA]])
        P.op("dve", lambda e: e.tensor_copy(out=sm[lo, I_FB, :], in_=sm[lo, I_T6, :]), reads=rd2, writes=[S_[I_FB]])
        P.op("dve", lambda e: e.tensor_copy(out=sm[hi, I_FB, :], in_=sm[hi, I_T7, :]), reads=rd2, writes=[S_[I_FB]])
        P.op("dve", lambda e: e.tensor_copy(out=sm[lo, I_FA2, :], in_=sm[lo, I_T6, :]), reads=rd2, writes=[S_[I_FA2]])
        P.op("dve", lambda e: e.tensor_copy(out=sm[hi, I_FA2, :], in_=sm[hi, I_FRE, :]), reads=rd2, writes=[S_[I_FA2]])
        P.op("dve", lambda e: e.tensor_copy(out=sm[lo, I_FB2, :], in_=sm[lo, I_T7, :]), reads=rd2, writes=[S_[I_FB2]])
        P.op("dve", lambda e: e.tensor_copy(out=sm[hi, I_FB2, :], in_=sm[hi, I_T6, :]), reads=rd2, writes=[S_[I_FB2]])
        yield
        ph = Phase(base=ARENA - 20 * 1024 // 2)
        cin = ph.alloc("cin", [128, 8, 2, 128], F32)
        cout = ph.alloc("cout", [128, 8, 2, 128], BF16)
        ctmp = ph.alloc("ctmp", [128, 128], F32)
        for b in range(8):
            P.dma("sp", lambda e, b=b: e.dma_start(out=cin.ap, in_=CD_d[:, b * 8:(b + 1) * 8, :, :]), writes=[cin])
            for gl in range(8):
                g = b * 8 + gl
                for v, (ia, ib) in enumerate(((I_FA, I_FB), (I_FA2, I_FB2))):
                    P.op("dve", lambda e, gl=gl, g=g, ib=ib: e.tensor_scalar(out=ctmp.ap, in0=cin.ap[:, gl, 1, :], scalar1=sm[:, ib, g:g + 1], scalar2=None, op0=ALU.mult),
                         reads=[cin, S_[ib]], writes=[ctmp])
                    P.op("dve", lambda e, gl=gl, g=g, ia=ia, v=v: e.scalar_tensor_tensor(out=cout.ap[:, gl, v, :], in0=cin.ap[:, gl, 0, :], scalar=sm[:, ia, g:g + 1],
                                                                                   in1=ctmp.ap, op0=ALU.mult, op1=ALU.add),
                         reads=[cin, S_[ia], ctmp], writes=[cout])
                if gl % 2 == 1:
                    yield
            P.dma("sp", lambda e, b=b: e.dma_start(out=Cs_d[:, b * 8:(b + 1) * 8, :, :], in_=cout.ap), reads=[cout], writes=[T_Cs[b]], sem_tile=T_Cs[b])
        ph = Phase(base=ARENA - 40 * 1024 // 2)
        q = ph.alloc("tq", [128, 2, 2, TC], F32)
        ki = ph.alloc("tki", [128, 2, 2, TC], I32)
        g1 = ph.alloc("tg1", [128, 2, 2, TC], F32)
        tb = [ph.alloc(f"ttb{i}", [128, 2, 2, TC], F32) for i in range(2)]
        fl = lambda t: t.ap.rearrange("p a b c -> p (a b c)")
        for gp in range(32):
            g0 = gp * 2
            for gl in range(2):
                g = g0 + gl
                P.op("act", lambda e, gl=gl, g=g: e.activation(out=q.ap[:, gl, 0, :], in_=iota[:], func=AF.Identity, scale=sm[:, I_TH, g:g + 1], bias=cst[:, 2:3]),
                     reads=[Tiota, S_[I_TH], Tcst], writes=[q])
                P.op("pool", lambda e, gl=gl, g=g: e.tensor_scalar(out=q.ap[:, gl, 1, :], in0=iota[:], scalar1=sm[:, I_TH, g:g + 1], scalar2=8.0, op0=ALU.mult, op1=ALU.add),
                     reads=[Tiota, S_[I_TH]], writes=[q])
            P.op("dve", lambda e: e.tensor_scalar(out=fl(g1), in0=fl(q), scalar1=12582912.0, scalar2=12582912.0, op0=ALU.add, op1=ALU.subtract), reads=[q], writes=[g1])
            yield
            P.op("dve", lambda e: e.tensor_tensor(out=fl(q), in0=fl(q), in1=fl(g1), op=ALU.subtract), reads=[q, g1], writes=[q])
            yield
            t_ = tb[gp % 2]
            P.op("act", lambda e, t_=t_: e.activation(out=t_.ap[:, :, 0, :], in_=q.ap[:, :, 0, :], func=AF.Sin, scale=TWO_PI), reads=[q], writes=[t_])
            P.op("act", lambda e, t_=t_: e.activation(out=t_.ap[:, :, 1, :], in_=q.ap[:, :, 1, :], func=AF.Sin, scale=col(C_SC2PI)), reads=[q, Tvec], writes=[t_])
            P.dma("sp", lambda e, t_=t_, g0=g0: e.dma_start(out=TAB_d[:, g0:g0 + 2, :, :], in_=t_.ap), reads=[t_], writes=[T_TAB[gp]], sem_tile=T_TAB[gp])
            yield

    def reduce_turns(qa, kia, g1a, Tq, Tki, Tg1):
        P.op("dve", lambda e: e.tensor_copy(out=kia, in_=qa), reads=Tq, writes=Tki)
        P.op("dve", lambda e: e.tensor_tensor(out=qa, in0=qa, in1=kia, op=ALU.subtract), reads=Tq + Tki, writes=Tq)
        P.op("dve", lambda e: e.scalar_tensor_tensor(out=g1a, in0=qa, scalar=0.5, in1=qa, op0=ALU.is_gt, op1=ALU.subtract), reads=Tq, writes=Tg1)
        P.op("dve", lambda e: e.scalar_tensor_tensor(out=qa, in0=g1a, scalar=0.5, in1=g1a, op0=ALU.is_gt, op1=ALU.subtract), reads=Tg1, writes=Tq)

    def mixer(tc):
        S_ = Tsm
        ph = Phase()
        yabf = ph.alloc("yabf", [128, 8, TC], BF16)
        yb = ph.alloc("yb", [128, 8, TC], BF16)
        Tya = [T(yabf.ap, f"ya{i}") for i in range(8)]
        Tyb = [T(yb.ap, f"yb{i}") for i in range(8)]
        for t in Tya:
            t.r = dict(yabf.r)
        for t in Tyb:
            t.r = dict(yb.r)
        base2 = ph.ptr
        v32 = [ph.alloc(f"v32_{i}", [128, TC], F32) for i in range(2)]
        vbf = [ph.alloc(f"vbf_{i}", [128, TC], BF16) for i in range(2)]
        tabs = [ph.alloc(f"tab{i}", [128, 2, TC], F32) for i in range(3)]
        t1 = [ph.alloc(f"t1_{i}", [128, TC], F32) for i in range(2)]
        t2 = [ph.alloc(f"t2_{i}", [128, TC], F32) for i in range(2)]
        Z = [ph.alloc(f"Z_{i}", [128, TC], F32) for i in range(2)]
        W1 = [ph.alloc(f"W1_{i}", [128, TC], BF16) for i in range(2)]
        W2 = [ph.alloc(f"W2_{i}", [128, TC], BF16) for i in range(2)]
        Cs = [ph.alloc(f"Cs_{i}", [128, 8, 2, 128], BF16) for i in range(2)]
        ysum = [ph.alloc(f"ysum{i}", [128, TC], F32) for i in range(2)]
        gtmp = [ph.alloc(f"gtmp{i}", [128, TC], F32) for i in range(2)]
        csb = ph.alloc("csb", [128, TC], F32)
        zt = ph.alloc("zt", [128, TC + 2], F32)
        cacc = ph.alloc("cacc", [128, TC], F32)
        CB = (6, 0, 7)
        CORD = (1, 2, 0)
        st = {}

        def setup_batch(cb):
            vslot = ws.get()
            vview = ring_t[vslot][:, 0:16 * 128].rearrange("p (kc m) -> p kc m", kc=16)
            for kc in range(16):
                P.op("pe", lambda e, kc=kc, vview=vview: e.matmul(pb[0].ap[:], vview[:, kc, 0:128], uT_t[:, kc, :], start=(kc == 0), stop=(kc == 15)),
                     reads=Tring[vslot] + TuT, writes=[pb[0]])
            vv, vb_ = v32[cb % 2], vbf[cb % 2]
            P.op("act", lambda e, vv=vv: e.activation(out=vv.ap, in_=pb[0].ap[:], func=AF.Copy), reads=[pb[0]], writes=[vv])
            P.op("act", lambda e, vb_=vb_, vv=vv: e.activation(out=vb_.ap, in_=vv.ap, func=AF.Copy), reads=[vv], writes=[vb_])
            bslot = ws.get()
            btv = ring_t[bslot][:, 0:8 * 2 * 128].rearrange("p (g v n) -> p g v n", g=8, v=2)
            cslot = ws.get()
            cv = ring_t[cslot][:, 0:16 * 384].rearrange("p (kc s m) -> p kc s m", kc=16, s=3)
            cs_ = Cs[cb % 2]
            P.dma("sp", lambda e, cs_=cs_, cb=cb: e.dma_start(out=cs_.ap, in_=Cs_d[:, cb * 8:(cb + 1) * 8, :, :]), reads=[T_Cs[cb]], writes=[cs_])
            st[cb] = dict(vv=vv, vb=vb_, bslot=bslot, btv=btv, cslot=cslot, cv=cv, cs=cs_)

        def stageA_pe(G, with_conv=True):
            cb, gl = G // 8, G % 8
            d = st[cb]
            vb_, btv, bslot = d["vb"], d["btv"], d["bslot"]
            tab = tabs[G % 3]
            P.dma("sp", lambda e, tab=tab, G=G: e.dma_start(out=tab.ap, in_=TAB_d[:, G, :, :]), reads=[T_TAB[G // 2]], writes=[tab])
            x1b, x2b = (1, 2) if G % 2 == 0 else (3, 4)
            P.op("pe", lambda e: e.matmul(pb[x1b].ap[:], btv[:, gl, 0, :], vb_.ap, start=True, stop=True), reads=Tring[bslot] + [vb_], writes=[pb[x1b]])
            P.op("pe", lambda e: e.matmul(pb[x2b].ap[:], btv[:, gl, 1, :], vb_.ap, start=True, stop=True), reads=Tring[bslot] + [vb_], writes=[pb[x2b]])

        def stageA_conv(G):
            cb, gl = G // 8, G % 8
            d = st[cb]
            cv, cslot = d["cv"], d["cslot"]
            for ci in range(gl * 6, gl * 6 + 6):
                s_i, kc = CORD[ci // 16], ci % 16
                P.op("pe", lambda e, s_i=s_i, kc=kc: e.matmul(pb[CB[s_i]].ap[:], cv[:, kc, s_i, :], uT_t[:, kc, :], start=(kc == 0), stop=(kc == 15)),
                     reads=Tring[cslot] + TuT, writes=[pb[CB[s_i]]])
                if ci == 15:
                    P.op("act", lambda e: e.activation(out=csb.ap, in_=pb[0].ap[:], func=AF.Copy), reads=[pb[0]], writes=[csb])

        def stageA_dve(G):
            tab = tabs[G % 3]
            x1b, x2b = (1, 2) if G % 2 == 0 else (3, 4)
            a1, a2 = t1[G % 2], t2[G % 2]
            P.op("dve", lambda e: e.tensor_tensor(out=a1.ap, in0=pb[x1b].ap[:], in1=tab.ap[:, 0, :], op=ALU.mult), reads=[pb[x1b], tab], writes=[a1])
            P.op("dve", lambda e: e.tensor_tensor(out=a2.ap, in0=pb[x2b].ap[:], in1=tab.ap[:, 1, :], op=ALU.mult), reads=[pb[x2b], tab], writes=[a2])

        def stageA_add(G):
            a1, a2 = t1[G % 2], t2[G % 2]
            P.op("dve", lambda e: e.tensor_tensor(out=a1.ap, in0=a1.ap, in1=a2.ap, op=ALU.add), reads=[a1, a2], writes=[a1])

        def stageB_ew(G):
            tab = tabs[G % 3]
            a1, z_, w1_, w2_ = t1[G % 2], Z[G % 2], W1[G % 2], W2[G % 2]
            P.op("dve", lambda e: e.tensor_tensor_scan(out=z_.ap, data0=sm[:, I_R, G:G + 1].broadcast_to([128, TC]), data1=a1.ap,
                                                       initial=sm[:, I_INIT, G:G + 1], op0=ALU.mult, op1=ALU.add),
                 reads=[a1, S_[I_R], S_[I_INIT]], writes=[z_])
            P.op("act", lambda e: e.activation(out=sm[:, I_LAST, G:G + 1], in_=z_.ap[:, TC - 1:TC], func=AF.Copy), reads=[z_], writes=[S_[I_LAST]])
            P.op("pool", lambda e: e.tensor_tensor(out=w2_.ap, in0=z_.ap, in1=tab.ap[:, 1, :], op=ALU.mult), reads=[z_, tab], writes=[w2_])
            P.op("pool", lambda e: e.tensor_tensor(out=w1_.ap, in0=z_.ap, in1=tab.ap[:, 0, :], op=ALU.mult), reads=[z_, tab], writes=[w1_])

        def stageB_pe(G):
            cb, gl = G // 8, G % 8
            cs_ = st[cb]["cs"]
            w1_, w2_ = W1[G % 2], W2[G % 2]
            P.op("pe", lambda e: e.matmul(pb[5].ap[:], cs_.ap[:, gl, 0, :], w1_.ap, start=(gl == 0), stop=False), reads=[cs_, w1_], writes=[pb[5]])
            P.op("pe", lambda e: e.matmul(pb[5].ap[:], cs_.ap[:, gl, 1, :], w2_.ap, start=False, stop=(gl == 7)), reads=[cs_, w2_], writes=[pb[5]])

        def epilogue(cb):
            vv = st[cb]["vv"]
            ys, gt = ysum[cb % 2], gtmp[cb % 2]
            P.op("dve", lambda e: e.scalar_tensor_tensor(out=ys.ap, in0=vv.ap, scalar=col(C_SD + cb), in1=pb[5].ap[:], op0=ALU.mult, op1=ALU.add),
                 reads=[vv, Tvec, pb[5]], writes=[ys])
            P.op("pool", lambda e: e.tensor_tensor(out=gt.ap, in0=ys.ap, in1=ys.ap, op=ALU.mult), reads=[ys], writes=[gt])
            P.op("pool", lambda e: e.tensor_scalar(out=gt.ap, in0=gt.ap, scalar1=0.044715, scalar2=1.0, op0=ALU.mult, op1=ALU.add), reads=[gt], writes=[gt])
            P.op("pool", lambda e: e.tensor_tensor(out=gt.ap, in0=gt.ap, in1=ys.ap, op=ALU.mult), reads=[ys, gt], writes=[gt])
            P.op("act", lambda e: e.activation(out=gt.ap, in_=gt.ap, func=AF.Sigmoid, scale=1.5957691216057308), reads=[gt], writes=[gt])
            P.op("pool", lambda e: e.tensor_tensor(out=yabf.ap[:, cb, :], in0=ys.ap, in1=gt.ap, op=ALU.mult), reads=[ys, gt], writes=[Tya[cb]])
            P.op("act", lambda e: e.activation(out=zt.ap[:, 0:2], in_=zcar[:, cb, :], func=AF.Copy), reads=[Tzcar[cb]], writes=[zt])
            P.op("dve", lambda e: e.tensor_tensor(out=zt.ap[:, 2:TC + 2], in0=pb[7].ap[:], in1=csb.ap, op=ALU.mult), reads=[pb[7], csb, zt], writes=[zt])
            P.op("act", lambda e: e.activation(out=zcar[:, cb, :], in_=zt.ap[:, TC:TC + 2], func=AF.Copy), reads=[zt], writes=[Tzcar[cb]])
            P.op("dve", lambda e: e.tensor_scalar(out=cacc.ap, in0=zt.ap[:, 2:TC + 2], scalar1=col(C_CW + 16 + cb), scalar2=col(C_CB + cb), op0=ALU.mult, op1=ALU.add),
                 reads=[zt, Tvec], writes=[cacc])
            P.op("dve", lambda e: e.scalar_tensor_tensor(out=cacc.ap, in0=zt.ap[:, 1:TC + 1], scalar=col(C_CW + 8 + cb), in1=cacc.ap, op0=ALU.mult, op1=ALU.add),
                 reads=[zt, Tvec, cacc], writes=[cacc])
            P.op("dve", lambda e: e.scalar_tensor_tensor(out=cacc.ap, in0=zt.ap[:, 0:TC], scalar=col(C_CW + cb), in1=cacc.ap, op0=ALU.mult, op1=ALU.add),
                 reads=[zt, Tvec, cacc], writes=[cacc])
            P.op("dve", lambda e: e.tensor_tensor(out=yb.ap[:, cb, :], in0=pb[6].ap[:], in1=cacc.ap, op=ALU.mult), reads=[pb[6], cacc], writes=[Tyb[cb]])
            if tc < NTC - 1:
                gs = slice(cb * 8, cb * 8 + 8)
                P.op("pe", lambda e: e.matmul(pb[7].ap[:, 0:8], perm[:], sm[:, I_LAST, gs], start=True, stop=True), reads=[Tperm, S_[I_LAST]], writes=[pb[7]])
                P.op("dve", lambda e: e.tensor_tensor(out=sm[:, I_T7, gs], in0=pb[7].ap[:, 0:8], in1=sm[:, I_SINL, gs], op=ALU.mult), reads=[pb[7], S_[I_SINL]], writes=[S_[I_T7]])
                P.op("dve", lambda e: e.tensor_tensor(out=sm[:, I_T6, gs], in0=sm[:, I_LAST, gs], in1=sm[:, I_COSL, gs], op=ALU.mult), reads=[S_[I_LAST], S_[I_COSL]], writes=[S_[I_T6]])
                P.op("dve", lambda e: e.tensor_tensor(out=sm[:, I_INIT, gs], in0=sm[:, I_T6, gs], in1=sm[:, I_T7, gs], op=ALU.add), reads=[S_[I_T6], S_[I_T7]], writes=[S_[I_INIT]])

        setup_batch(0)
        stageA_pe(0)
        stageA_conv(0)
        stageA_dve(0)
        stageA_add(0)
        for i in range(64):
            if i + 1 < 64:
                if (i + 1) % 8 == 0:
                    setup_batch((i + 1) // 8)
                stageA_pe(i + 1)
            if i >= 1:
                stageB_pe(i - 1)
            if i + 1 < 64:
                stageA_conv(i + 1)
            if i + 1 < 64:
                stageA_dve(i + 1)
            stageB_ew(i)
            if i + 1 < 64:
                stageA_add(i + 1)
            if i >= 1 and (i - 1) % 8 == 7:
                epilogue((i - 1) // 8)
        stageB_pe(63)
        epilogue(7)
        ph2 = Phase(base=base2)
        yg = ph2.alloc("yg", [128, 8, TC], BF16)
        Tyg = [T(yg.ap, f"yg{i}") for i in range(8)]
        for t in Tyg:
            t.r = dict(yg.r)
        gl_ = [ph2.alloc(f"gl{i}", [128, TC], F32) for i in range(2)]
        sa = [ph2.alloc(f"sa{i}", [128, TC], F32) for i in range(2)]
        sbb = [ph2.alloc(f"sbb{i}", [128, TC], F32) for i in range(2)]
        m1 = [ph2.alloc(f"m1{i}", [128, TC], F32) for i in range(2)]
        m2 = [ph2.alloc(f"m2{i}", [128, TC], F32) for i in range(2)]
        mg = ph2.alloc("merged", [128, 16, TC], BF16)
        Tmg = [T(mg.ap, f"mg{i}") for i in range(16)]
        for t in Tmg:
            t.r = dict(mg.r)
        mcnt = 0
        for (m0, n) in VSUB:
            gslot = ws.get()
            gv = ring_t[gslot][:, 0:8 * n * 128].rearrange("p (kc m) -> p kc m", kc=8)
            for q_ in range(n):
                m = m0 + q_
                bk = mcnt % 2
                for kc in range(8):
                    P.op("pe", lambda e, bk=bk, kc=kc, q_=q_, gv=gv: e.matmul(pb[bk].ap[:], gv[:, kc, q_ * 128:(q_ + 1) * 128], yabf.ap[:, kc, :], start=(kc == 0), stop=(kc == 7)),
                         reads=Tring[gslot] + Tya, writes=[pb[bk]])
                g_ = gl_[mcnt % 2]
                P.op("act", lambda e, g_=g_, bk=bk, m=m: e.activation(out=g_.ap, in_=pb[bk].ap[:], func=AF.Sigmoid, bias=col(C_BG + m)), reads=[pb[bk], Tvec], writes=[g_])
                P.op("dve", lambda e, g_=g_, m=m: e.tensor_tensor(out=yg.ap[:, m, :], in0=yabf.ap[:, m, :], in1=g_.ap, op=ALU.mult), reads=[Tya[m], g_], writes=[Tyg[m]])
                mcnt += 1
        for i in range(16):
            mslot = ws.get()
            r = ring_t[mslot]
            woc = r[:, 0:2048].rearrange("p (kc m) -> p kc m", kc=8)
            wa, wb = woc[:, :, 0:128], woc[:, :, 128:256]
            wgt = r[:, 2048:6144].rearrange("p (kc m) -> p kc m", kc=16)
            wga, wgb = wgt[:, :, 0:128], wgt[:, :, 128:256]
            o = 0 if i % 2 == 0 else 4
            for kc in range(16):
                P.op("pe", lambda e, o=o, kc=kc, wga=wga: e.matmul(pb[o + 2].ap[:], wga[:, kc, :], uT_t[:, kc, :], start=(kc == 0), stop=(kc == 15)), reads=Tring[mslot] + TuT, writes=[pb[o + 2]])
            for kc in range(16):
                P.op("pe", lambda e, o=o, kc=kc, wgb=wgb: e.matmul(pb[o + 3].ap[:], wgb[:, kc, :], uT_t[:, kc, :], start=(kc == 0), stop=(kc == 15)), reads=Tring[mslot] + TuT, writes=[pb[o + 3]])
            for kc in range(8):
                P.op("pe", lambda e, o=o, kc=kc, wa=wa: e.matmul(pb[o].ap[:], wa[:, kc, :], yg.ap[:, kc, :], start=(kc == 0), stop=(kc == 7)), reads=Tring[mslot] + Tyg, writes=[pb[o]])
            for kc in range(8):
                P.op("pe", lambda e, o=o, kc=kc, wb=wb: e.matmul(pb[o + 1].ap[:], wb[:, kc, :], yb.ap[:, kc, :], start=(kc == 0), stop=(kc == 7)), reads=Tring[mslot] + Tyb, writes=[pb[o + 1]])
            sa_, sb_, m1_, m2_ = sa[i % 2], sbb[i % 2], m1[i % 2], m2[i % 2]
            P.op("act", lambda e, sa_=sa_, o=o: e.activation(out=sa_.ap, in_=pb[o + 2].ap[:], func=AF.Sigmoid), reads=[pb[o + 2]], writes=[sa_])
            P.op("act", lambda e, sb_=sb_, o=o: e.activation(out=sb_.ap, in_=pb[o + 3].ap[:], func=AF.Sigmoid), reads=[pb[o + 3]], writes=[sb_])
            P.op("dve", lambda e, m1_=m1_, sa_=sa_, o=o: e.tensor_tensor(out=m1_.ap, in0=pb[o].ap[:], in1=sa_.ap, op=ALU.mult), reads=[pb[o], sa_], writes=[m1_])
            P.op("dve", lambda e, m2_=m2_, sb_=sb_, o=o: e.tensor_tensor(out=m2_.ap, in0=pb[o + 1].ap[:], in1=sb_.ap, op=ALU.mult), reads=[pb[o + 1], sb_], writes=[m2_])
            P.op("dve", lambda e, m1_=m1_, m2_=m2_, i=i: e.tensor_tensor(out=mg.ap[:, i, :], in0=m1_.ap, in1=m2_.ap, op=ALU.add), reads=[m1_, m2_], writes=[Tmg[i]])
        for j in range(4):
            sA = ws.get()
            sB = ws.get()
            vA = ring_t[sA][:, 0:8 * 512].rearrange("p (fc c) -> p fc c", fc=8)
            vB = ring_t[sB][:, 0:8 * 512].rearrange("p (fc c) -> p fc c", fc=8)
            o = 0 if j % 2 == 0 else 4
            for i in range(4):
                for kc in range(16):
                    vw, sl = (vA, sA) if kc < 8 else (vB, sB)
                    P.op("pe", lambda e, o=o, i=i, kc=kc, vw=vw: e.matmul(pb[o + i].ap[:], mg.ap[:, kc, i * 128:(i + 1) * 128], vw[:, kc % 8, :], start=(kc == 0), stop=(kc == 15)),
                         reads=[Tmg[kc]] + Tring[sl], writes=[pb[o + i]])
                P.op("dve", lambda e, o=o, i=i, j=j: e.tensor_tensor(out=h_t[i][:, j * 512:(j + 1) * 512], in0=pb[o + i].ap[:], in1=h_t[i][:, j * 512:(j + 1) * 512], op=ALU.add),
                     reads=[pb[o + i], Th[i]], writes=[Th[i]])
        for parent, subs in ((yabf, Tya), (yb, Tyb), (yg, Tyg), (mg, Tmg)):
            for t in subs:
                if t.w is not None and parent.r.get(t.w[0], 0) < t.w[1]:
                    parent.r[t.w[0]] = t.w[1]
                for s_, v in t.r.items():
                    if parent.r.get(s_, 0) < v:
                        parent.r[s_] = v

    setup_gen = ssm_setup() if (do_mix or do_setup_only) else None
    xs = None
    for i in range(4):
        P.dma("sp", lambda e, i=i: e.dma_start(out=h_t[i][:], in_=x[i * 128:(i + 1) * 128, :]), writes=[Th[i]])
    pending_final = None
    for tc in range(NTC):
        if do_ffn1 and tc == 0:
            norm_T(C_G1)
            ffn(side_work=setup_gen)
        elif do_ffn1:
            norm_T(C_G1, src=xs)
            ffn(side_work=pending_final, src=xs, drain_first=True)
        elif tc == 0:
            if setup_gen is not None:
                for _ in setup_gen:
                    pass
        else:
            if pending_final is not None:
                for _ in pending_final:
                    pass
            for i in range(4):
                P.op("act", lambda e, i=i: e.activation(out=h_t[i][:], in_=xs[i].ap, func=AF.Copy), reads=[xs[i]], writes=[Th[i]])
        pending_final = None
        if do_mix:
            norm_T(C_GM)
            mixer(tc)
        if do_ffn2:
            norm_T(C_G2)
            if tc + 1 < NTC:
                xs = prefetch_x(tc + 1)
            ffn()
        elif tc + 1 < NTC:
            xs = prefetch_x(tc + 1)
        pending_final = final_gen(tc)
    for _ in pending_final:
        pass
    P.wait_all("sp", out_evs)
    print("NSEMS", len(P.sems), {e: len(P.q[e]) for e in P.ENG}, flush=True)
    P.emit()
    return nc


def prep_shared(inp):
    f32 = np.float32
    vec = np.zeros((128, 128), f32)

    def cols(v, n):
        return np.ascontiguousarray(np.asarray(v, f32).reshape(n, 128).T)

    vec[:, 0:16] = cols(inp["ffn1_norm"], 16)
    vec[:, 16:32] = cols(inp["mix_norm"], 16)
    vec[:, 32:48] = cols(inp["ffn2_norm"], 16)
    cw = np.asarray(inp["conv_w"], f32)
    for k in range(3):
        vec[:, 48 + 8 * k:56 + 8 * k] = cols(cw[k], 8)
    vec[:, 72:80] = cols(inp["conv_b"], 8)
    vec[:, 80:88] = cols(inp["ssm_d"], 8)
    vec[:, 88:96] = cols(inp["ssm_b_glu"], 8)
    vec[0:64, 96] = 1.0
    vec[64:128, 96] = -1.0
    vec[0:64, 97] = -1.0
    vec[64:128, 97] = 1.0
    vec[0:64, 98] = 2 * np.pi
    vec[64:128, 98] = -2 * np.pi
    vec[0:64, 99] = -2 * np.pi
    vec[64:128, 99] = 2 * np.pi
    gfin = np.ascontiguousarray(np.broadcast_to(np.asarray(inp["final_norm"], f32), (128, D)))
    lam = np.zeros((128, 3, 64), f32)
    lre = np.asarray(inp["ssm_lambda_re"], f32).T
    lim = np.asarray(inp["ssm_lambda_im"], f32).T
    lam[0:64, 0], lam[64:128, 0] = lre, lre
    lam[0:64, 1], lam[64:128, 1] = lim, lim
    lam[:, 2, :] = np.asarray(inp["ssm_log_dt"], f32)[None, :]
    bre = np.asarray(inp["ssm_b_re"], f32)
    bim = np.asarray(inp["ssm_b_im"], f32)
    cre = np.asarray(inp["ssm_c_re"], f32)
    cim = np.asarray(inp["ssm_c_im"], f32)
    BT = np.zeros((128, 64, 2, 128), f32)
    CD = np.zeros((128, 64, 2, 128), f32)
    for g in range(64):
        r0 = (g % 8) * 16
        BT[r0:r0 + 16, g, 0, 0:64] = bre[g].T
        BT[r0:r0 + 16, g, 0, 64:128] = bim[g].T
        BT[r0:r0 + 16, g, 1, 0:64] = bim[g].T
        BT[r0:r0 + 16, g, 1, 64:128] = bre[g].T
        CD[0:64, g, 0, r0:r0 + 16] = cre[g].T
        CD[64:128, g, 0, r0:r0 + 16] = cre[g].T
        CD[0:64, g, 1, r0:r0 + 16] = cim[g].T
        CD[64:128, g, 1, r0:r0 + 16] = cim[g].T
    perm = np.zeros((128, 128), f32)
    for n in range(64):
        perm[64 + n, n] = -1.0
        perm[n, 64 + n] = 1.0
    w_in_h = np.asarray(inp["w_in"], f32)
    w_cv = np.ascontiguousarray(w_in_h[:, 1024:4096].reshape(D, 3, 8, 128).transpose(0, 2, 1, 3).reshape(D, 3072))
    w_gate = np.ascontiguousarray(w_in_h[:, 4096:8192].reshape(D, 2, 16, 128).transpose(0, 2, 1, 3).reshape(D, 4096))
    w_oc = np.ascontiguousarray(np.stack([np.asarray(inp["ssm_w_out"], f32).reshape(W, 16, 128),
                                          np.asarray(inp["conv_w_out"], f32).reshape(W, 16, 128)], axis=2).reshape(W, 4096))
    sh = dict(
        w_cv=w_cv, w_gate=w_gate, w_oc=w_oc,
        w1g=np.asarray(inp["ffn1_w_gate"], f32), w1u=np.asarray(inp["ffn1_w_up"], f32), w1d=np.asarray(inp["ffn1_w_down"], f32),
        w2g=np.asarray(inp["ffn2_w_gate"], f32), w2u=np.asarray(inp["ffn2_w_up"], f32), w2d=np.asarray(inp["ffn2_w_down"], f32),
        w_in=np.asarray(inp["w_in"], f32), w_glu=np.asarray(inp["ssm_w_glu"], f32), w_o=np.asarray(inp["w_o"], f32),
        vec=vec, gfin=gfin, lam=lam, BT=BT, CD=CD, ident=np.eye(128, dtype=f32), perm=perm,
        iota=np.ascontiguousarray(np.broadcast_to(np.arange(TC, dtype=f32), (128, TC))),
    )
    return sh


_NC_CACHE = {}


def run(inputs, stage="full", cores=8):
    if stage not in _NC_CACHE:
        _NC_CACHE[stage] = build(stage)
    nc = _NC_CACHE[stage]
    sh = prep_shared(inputs)
    xs = np.asarray(inputs["x"], np.float32)
    in_maps = []
    for c in range(cores):
        m = dict(sh)
        m["x"] = np.ascontiguousarray(xs[c])
        in_maps.append(m)
    res = run_bass_kernel_spmd(nc, in_maps, core_ids=list(range(cores)))
    return np.stack([np.asarray(r["out"]) for r in res.results], 0)


def kernel(**inputs):
    return run(inputs, "full", 8).astype(np.float32)
```
